# Optimizing a Trainium2 kernel written in Bass

```python
import math
import jax, jax.numpy as jnp
from jax import lax
import numpy as np

D_MODEL = 1024
BATCH = 8
SEQ = 4096
DEPTH = 4

N_META = 16
GRID_W = 64
Q_BLOCK = 128
HEAD_DIM = 64
A_HEADS = D_MODEL // 256
A_VDIM = 2 * HEAD_DIM
B_HEADS = D_MODEL // 128
B_KV_HEADS = 2
C_HEADS = D_MODEL // 64
C_KV_HEADS = 2
WINDOW = 128
D_FF = 4 * D_MODEL
REL_BUCKETS = 32
REL_MAX_DIST = 128
REL_HEADS = A_HEADS + C_HEADS
ROPE_THETA = 10000.0
ROPE_AXIS_DIM = HEAD_DIM // 2
EPS = 1e-6
NEG_INF = -1e30
N_EVEN = (DEPTH + 1) // 2
N_ODD = DEPTH // 2

A_Q = A_HEADS * 2 * HEAD_DIM
A_K = A_HEADS * 2 * HEAD_DIM
A_V = A_HEADS * A_VDIM
B_Q = B_HEADS * HEAD_DIM
B_K = B_KV_HEADS * HEAD_DIM
B_V = B_KV_HEADS * HEAD_DIM
EVEN_SPLITS = (A_Q, A_Q + A_K, A_Q + A_K + A_V, A_Q + A_K + A_V + B_Q, A_Q + A_K + A_V + B_Q + B_K)
EVEN_IN = A_Q + A_K + A_V + B_Q + B_K + B_V
EVEN_MIX = A_HEADS * A_VDIM + B_HEADS * HEAD_DIM
C_Q = C_HEADS * HEAD_DIM
C_K = C_KV_HEADS * HEAD_DIM
ODD_SPLITS = (C_Q, C_Q + C_K)
ODD_IN = C_Q + 2 * C_K
ODD_MIX = C_HEADS * HEAD_DIM

kernel_name = "hybrid_diffattn_axialgqa_swa_sink_encoder"


def _rmsnorm(x, g):
    xf = x.astype(jnp.float32)
    y = xf * lax.rsqrt(jnp.mean(xf * xf, axis=-1, keepdims=True) + EPS)
    return (y * g.astype(jnp.float32)).astype(x.dtype)


def _rel_bucket(rel):
    nb = REL_BUCKETS // 2
    max_exact = nb // 2
    n = jnp.abs(rel)
    nf = jnp.maximum(n, 1).astype(jnp.float32)
    large = max_exact + (jnp.log(nf / max_exact) / math.log(REL_MAX_DIST / max_exact)
                         * (nb - max_exact)).astype(jnp.int32)
    large = jnp.minimum(large, nb - 1)
    return jnp.where(rel > 0, nb, 0) + jnp.where(n < max_exact, n, large)


def _rel_bias(table, q_pos, k_pos, lo, hi):
    buckets = _rel_bucket(k_pos[None, :] - q_pos[:, None])
    b = table[:, lo:hi].astype(jnp.float32)[buckets]
    return jnp.moveaxis(b, -1, 0)


def _sweep(fn, q):
    B = q.shape[0]
    S = q.shape[1] - N_META
    nb = S // Q_BLOCK
    meta_out = fn(q[:, :N_META], jnp.int32(0))
    qb = jnp.moveaxis(q[:, N_META:].reshape((B, nb, Q_BLOCK) + q.shape[2:]), 1, 0)
    starts = N_META + Q_BLOCK * jnp.arange(nb, dtype=jnp.int32)
    out = lax.map(lambda a: fn(a[0], a[1]), (qb, starts))
    out = jnp.moveaxis(out, 0, 1).reshape((B, S) + out.shape[3:])
    return jnp.concatenate([meta_out, out], axis=1)


def _axial_angles(S):
    ROWS = S // GRID_W
    rows = jnp.repeat(jnp.arange(ROWS, dtype=jnp.int32), GRID_W)
    cols = jnp.tile(jnp.arange(GRID_W, dtype=jnp.int32), ROWS)
    zeros = jnp.zeros((N_META,), jnp.int32)
    rows = jnp.concatenate([zeros, rows]).astype(jnp.float32)
    cols = jnp.concatenate([zeros, cols]).astype(jnp.float32)
    inv = ROPE_THETA ** (-jnp.arange(0, ROPE_AXIS_DIM, 2, dtype=jnp.float32) / ROPE_AXIS_DIM)
    return rows[:, None] * inv, cols[:, None] * inv


def _rotate(x, ang):
    shape = (1, ang.shape[0]) + (1,) * (x.ndim - 3) + (ang.shape[1],)
    cos = jnp.cos(ang).reshape(shape).astype(x.dtype)
    sin = jnp.sin(ang).reshape(shape).astype(x.dtype)
    x1, x2 = jnp.split(x, 2, axis=-1)
    return jnp.concatenate([x1 * cos - x2 * sin, x2 * cos + x1 * sin], axis=-1)


def _axial_rope(x, ang_r, ang_c):
    return jnp.concatenate([_rotate(x[..., :ROPE_AXIS_DIM], ang_r),
                            _rotate(x[..., ROPE_AXIS_DIM:], ang_c)], axis=-1)


def _diff_attention(q, k, v, lam, table, pos):
    scale = HEAD_DIM ** -0.5

    def block(qb, start):
        qpos = start + jnp.arange(qb.shape[1], dtype=jnp.int32)
        bias = _rel_bias(table, qpos, pos, 0, A_HEADS)
        s = jnp.einsum('bqhcd,bshcd->bhcqs', qb, k).astype(jnp.float32) * scale + bias[None, :, None]
        p = jax.nn.softmax(s, axis=-1)
        attn = p[:, :, 0] - lam * p[:, :, 1]
        return jnp.einsum('bhqs,bshv->bqhv', attn.astype(v.dtype), v)

    return _sweep(block, q)


def _gqa_dense(q, k, v):
    scale = HEAD_DIM ** -0.5

    def block(qb, start):
        s = jnp.einsum('bqkgd,bskd->bkgqs', qb, k).astype(jnp.float32) * scale
        p = jax.nn.softmax(s, axis=-1)
        return jnp.einsum('bkgqs,bskd->bqkgd', p.astype(v.dtype), v)

    return _sweep(block, q)


def _window_gqa(q, k, v, sinks, table):
    S = q.shape[1] - N_META
    scale = HEAD_DIM ** -0.5
    k_meta, v_meta = k[:, :N_META], v[:, :N_META]
    pad = ((0, 0), (WINDOW, WINDOW), (0, 0), (0, 0))
    k_pad = jnp.pad(k[:, N_META:], pad)
    v_pad = jnp.pad(v[:, N_META:], pad)
    meta_pos = jnp.arange(N_META, dtype=jnp.int32)
    sink = sinks.astype(jnp.float32).reshape(C_KV_HEADS, C_HEADS // C_KV_HEADS)

    def block(qb, start):
        nq = qb.shape[1]
        nk = nq + 2 * WINDOW
        qpos = start + jnp.arange(nq, dtype=jnp.int32)
        lo = jnp.maximum(start - N_META, 0)
        kb = lax.dynamic_slice_in_dim(k_pad, lo, nk, axis=1)
        vb = lax.dynamic_slice_in_dim(v_pad, lo, nk, axis=1)
        real_idx = lo - WINDOW + jnp.arange(nk, dtype=jnp.int32)
        band_pos = N_META + real_idx
        band_ok = ((real_idx >= 0) & (real_idx < S))[None, :] & \
                  (jnp.abs(band_pos[None, :] - qpos[:, None]) <= WINDOW)
        visible = jnp.concatenate([jnp.ones((nq, N_META), bool), band_ok], axis=1)
        kc = jnp.concatenate([k_meta, kb], axis=1)
        vc = jnp.concatenate([v_meta, vb], axis=1)
        kpos = jnp.concatenate([meta_pos, band_pos])
        bias = _rel_bias(table, qpos, kpos, A_HEADS, REL_HEADS)
        bias = bias.reshape(C_KV_HEADS, C_HEADS // C_KV_HEADS, nq, kpos.shape[0])
        s = jnp.einsum('bqkgd,bskd->bkgqs', qb, kc).astype(jnp.float32) * scale + bias[None]
        s = jnp.where(visible, s, NEG_INF)
        sink_col = jnp.broadcast_to(sink[None, :, :, None, None], s.shape[:-1] + (1,))
        p = jax.nn.softmax(jnp.concatenate([s, sink_col], axis=-1), axis=-1)[..., :-1]
        return jnp.einsum('bkgqs,bskd->bqkgd', p.astype(vc.dtype), vc)

    return _sweep(block, q)


def _even_mixer(h, layer_idx, w_in, w_out, lam_vecs, subln, qk_norm, table, pos, ang_r, ang_c):
    B, L, _ = h.shape
    qa, ka, va, qb, kb, vb = jnp.split(h @ w_in, EVEN_SPLITS, axis=-1)
    qa = qa.reshape(B, L, A_HEADS, 2, HEAD_DIM)
    ka = ka.reshape(B, L, A_HEADS, 2, HEAD_DIM)
    va = va.reshape(B, L, A_HEADS, A_VDIM)
    lam_init = 0.8 - 0.6 * math.exp(-0.3 * layer_idx)
    lv = lam_vecs.astype(jnp.float32)
    lam = jnp.exp(jnp.sum(lv[0] * lv[1])) - jnp.exp(jnp.sum(lv[2] * lv[3])) + lam_init
    oa = _diff_attention(qa, ka, va, lam, table, pos)
    oa = _rmsnorm(oa, subln) * (1.0 - lam_init)
    qb = _rmsnorm(qb.reshape(B, L, B_KV_HEADS, B_HEADS // B_KV_HEADS, HEAD_DIM), qk_norm[0])
    kb = _rmsnorm(kb.reshape(B, L, B_KV_HEADS, HEAD_DIM), qk_norm[1])
    vb = vb.reshape(B, L, B_KV_HEADS, HEAD_DIM)
    qb = _axial_rope(qb, ang_r, ang_c)
    kb = _axial_rope(kb, ang_r, ang_c)
    ob = _gqa_dense(qb, kb, vb)
    mix = jnp.concatenate([oa.reshape(B, L, -1), ob.reshape(B, L, -1)], axis=-1)
    return mix @ w_out


def _odd_mixer(h, w_in, w_out, sinks, table):
    B, L, _ = h.shape
    qc, kc, vc = jnp.split(h @ w_in, ODD_SPLITS, axis=-1)
    qc = qc.reshape(B, L, C_KV_HEADS, C_HEADS // C_KV_HEADS, HEAD_DIM)
    kc = kc.reshape(B, L, C_KV_HEADS, HEAD_DIM)
    vc = vc.reshape(B, L, C_KV_HEADS, HEAD_DIM)
    oc = _window_gqa(qc, kc, vc, sinks, table)
    return oc.reshape(B, L, ODD_MIX) @ w_out


def _sq_relu_mlp(h, w_up, w_down):
    u = jax.nn.relu(h @ w_up)
    return (u * u) @ w_down


def setup_inputs(seed: int = 0) -> dict:
    key = jax.random.key(seed)
    ks = jax.random.split(key, 16)
    f32 = jnp.float32
    nrm = lambda k, s: jax.random.normal(k, s, f32)
    return {
        "x": nrm(ks[0], (BATCH, SEQ, D_MODEL)),
        "meta_tokens": nrm(ks[1], (N_META, D_MODEL)),
        "rel_table": 0.5 * nrm(ks[2], (REL_BUCKETS, REL_HEADS)),
        "norm_attn": 1.0 + 0.05 * nrm(ks[3], (DEPTH, D_MODEL)),
        "norm_mlp": 1.0 + 0.05 * nrm(ks[4], (DEPTH, D_MODEL)),
        "norm_final": 1.0 + 0.05 * nrm(ks[5], (D_MODEL,)),
        "w_in_even": nrm(ks[6], (N_EVEN, D_MODEL, EVEN_IN)) * D_MODEL ** -0.5,
        "w_out_even": nrm(ks[7], (N_EVEN, EVEN_MIX, D_MODEL)) * EVEN_MIX ** -0.5,
        "diff_lambda": 0.1 * nrm(ks[8], (N_EVEN, 4, HEAD_DIM)),
        "diff_subln": 1.0 + 0.05 * nrm(ks[9], (N_EVEN, A_VDIM)),
        "qk_norm": 1.0 + 0.05 * nrm(ks[10], (N_EVEN, 2, HEAD_DIM)),
        "w_in_odd": nrm(ks[11], (N_ODD, D_MODEL, ODD_IN)) * D_MODEL ** -0.5,
        "w_out_odd": nrm(ks[12], (N_ODD, ODD_MIX, D_MODEL)) * ODD_MIX ** -0.5,
        "sinks": 0.5 * nrm(ks[13], (N_ODD, C_HEADS)),
        "w_up": nrm(ks[14], (DEPTH, D_MODEL, D_FF)) * D_MODEL ** -0.5,
        "w_down": nrm(ks[15], (DEPTH, D_FF, D_MODEL)) * D_FF ** -0.5,
    }


def reference(x, meta_tokens, rel_table, norm_attn, norm_mlp, norm_final, w_in_even, w_out_even,
              diff_lambda, diff_subln, qk_norm, w_in_odd, w_out_odd, sinks, w_up, w_down):
    B, S, _ = x.shape
    meta = jnp.broadcast_to(meta_tokens.astype(x.dtype)[None], (B, N_META, D_MODEL))
    h = jnp.concatenate([meta, x], axis=1)
    pos = jnp.arange(h.shape[1], dtype=jnp.int32)
    ang_r, ang_c = _axial_angles(S)
    for i in range(DEPTH):
        hn = _rmsnorm(h, norm_attn[i])
        if i % 2 == 0:
            e = i // 2
            h = h + _even_mixer(hn, i, w_in_even[e], w_out_even[e], diff_lambda[e], diff_subln[e],
                                qk_norm[e], rel_table, pos, ang_r, ang_c)
        else:
            o = i // 2
            h = h + _odd_mixer(hn, w_in_odd[o], w_out_odd[o], sinks[o], rel_table)
        h = h + _sq_relu_mlp(_rmsnorm(h, norm_mlp[i]), w_up[i], w_down[i])
    h = _rmsnorm(h, norm_final)
    return h[:, N_META:]
```

```python
import math
import os
import numpy as np
import ml_dtypes
import concourse.bass as bass
import concourse.mybir as mybir
from concourse.bass_utils import run_bass_kernel_spmd

F32 = mybir.dt.float32
BF16 = mybir.dt.bfloat16
AF = mybir.ActivationFunctionType
ALU = mybir.AluOpType

L = 4112
S = 4096
NM = 16
DM = 1024
DFF = 4096
EPS = 1e-6
CH512 = [(i * 512, 512) for i in range(8)] + [(4096, 16)]
CH256 = [(i * 256, 256) for i in range(16)] + [(4096, 16)]
KTL = [(i * 128, 128) for i in range(32)] + [(4096, 16)]
NW_EVEN = 14 * 128 + 640
NW_ODD = 10 * 128 + 128
RB = 1024


def tok_pos(t):
    return t + 16 if t < S else t - S


def rel_bucket_np(rel):
    nb = 16
    max_exact = 8
    n = np.abs(rel)
    nf = np.maximum(n, 1).astype(np.float32)
    large = max_exact + (np.log(nf / np.float32(max_exact)) / np.float32(math.log(128 / 8))
                         * np.float32(nb - max_exact)).astype(np.int32)
    large = np.minimum(large, nb - 1)
    return np.where(rel > 0, nb, 0) + np.where(n < max_exact, n, large)


class Trk:
    K = 8

    def __init__(self, nc, sems):
        self.nc = nc
        self.E = {'pe': nc.tensor, 'act': nc.scalar, 'dve': nc.vector, 'pool': nc.gpsimd, 'sp': nc.sync}
        self.sem = {e: sems['c_' + e] for e in self.E}
        self.cnt = {e: 0 for e in self.E}
        self.pend = {e: False for e in self.E}
        self.dsem = {q: [sems['d_%s_%d' % (q, i)] for i in range(self.K)] for q in ('sp', 'pool')}
        self.dval = {q: [0] * self.K for q in ('sp', 'pool')}
        self.dnext = {q: 0 for q in ('sp', 'pool')}
        self.seen = {e: {} for e in self.E}
        self.res = {}
        self.ninstr = 0
        self.nops = {e: 0 for e in self.E}
        self.nwaits = {e: 0 for e in self.E}

    def _wait(self, e, tok):
        if tok[0] == 'c':
            key = tok[1]
            if tok[1] == e and e in ('pe', 'sp'):
                return
            if tok[2] > self.cnt[tok[1]]:
                raise RuntimeError('wait on unresolved token %r (cnt %d)' % (tok, self.cnt[tok[1]]))
            sem = self.sem[tok[1]]
        else:
            key = (tok[1], tok[2])
            sem = self.dsem[tok[1]][tok[2]]
        val = tok[-1]
        if self.seen[e].get(key, 0) >= val:
            return
        self.E[e].wait_ge(sem, val)
        self.seen[e][key] = val
        self.ninstr += 1
        self.nwaits[e] += 1

    def _deps(self, r, w):
        deps = []
        for name in r:
            x = self.res.get(name)
            if x is not None and x[0] is not None:
                deps.append(x[0])
        for name in w:
            x = self.res.get(name)
            if x is not None:
                if x[0] is not None:
                    deps.append(x[0])
                deps.extend(x[1].values())
        return deps

    def _record(self, tok, r, w):
        key = tok[1] if tok[0] == 'c' else (tok[1], tok[2])
        for name in w:
            self.res[name] = [tok, {}]
        for name in r:
            x = self.res.get(name)
            if x is None:
                x = [None, {}]
                self.res[name] = x
            x[1][key] = tok

    def op(self, e, fn, r=(), w=(), sig=True):
        for tok in self._deps(r, w):
            self._wait(e, tok)
        ins = fn()
        self.ninstr += 1
        self.nops[e] += 1
        if sig:
            ins.then_inc(self.sem[e], 1)
            self.cnt[e] += 1
            self.pend[e] = False
            tok = ('c', e, self.cnt[e])
        else:
            self.pend[e] = True
            tok = ('c', e, self.cnt[e] + 1)
        self._record(tok, r, w)
        return tok

    def dma(self, q, out, in_, r=(), w=()):
        for tok in self._deps(r, w):
            self._wait(q, tok)
        slot = self.dnext[q] % self.K
        self.dnext[q] += 1
        if self.dval[q][slot] > 0:
            self._wait(q, ('d', q, slot, self.dval[q][slot]))
        self.E[q].dma_start(out=out, in_=in_).then_inc(self.dsem[q][slot], 16)
        self.ninstr += 1
        self.nops[q] += 1
        self.dval[q][slot] += 16
        tok = ('d', q, slot, self.dval[q][slot])
        self._record(tok, r, w)
        return tok

    def barrier(self):
        for e in self.E:
            assert not self.pend[e], e
        toks = [('c', e, self.cnt[e]) for e in self.E if self.cnt[e] > 0]
        for q in ('sp', 'pool'):
            for i in range(self.K):
                if self.dval[q][i] > 0:
                    toks.append(('d', q, i, self.dval[q][i]))
        for e in self.E:
            for tok in toks:
                self._wait(e, tok)
        self.res = {}


def build_program(n_layers=4, dbg=False, stop=None, layers=None, total_layers=4):
    nc = bass.Bass("TRN2", target_bir_lowering=False)
    dr = {}

    def din(name, shape, dt=F32):
        dr[name] = nc.dram_tensor(name, list(shape), dt, kind="ExternalInput")
        return dr[name]

    if layers is None:
        layers = list(range(int(os.environ.get('START_LAYER', '0')), n_layers))
        total_layers = n_layers
    first = (layers[0] == 0) or bool(os.environ.get('START_LAYER'))
    has_last = (layers[-1] == total_layers - 1)
    if first:
        x_d = din("x", [S, DM])
        meta_d = din("meta", [NM, DM])
    else:
        hin_d = din("hT_in", [DM, L])
    tab_d = din("rel_table", [32, 20])
    pvec_d = din("pvec", [128, 96])
    lam_d = din("diff_lambda", [1, 512])
    sinks_d = din("sinks", [1, 32])
    for ly in layers:
        din("w_in%d" % ly, [DM, NW_EVEN if ly % 2 == 0 else NW_ODD])
        din("w_out%d" % ly, [DM, DM])
        din("w_up%d" % ly, [DM, DFF])
        din("w_down%d" % ly, [DFF, DM])
    cmat_d = din("cmat", [4, 128, 128])
    oh_d = din("onehot", [32, 2048])
    msk_d = din("bandmask", [20, 2048])
    rope_d = din("rope", [2, 128, L])
    if has_last:
        out_d = nc.dram_tensor("out", [S, DM], F32, kind="ExternalOutput")
    hT_d = nc.dram_tensor("hT", [DM, L], F32, **({"kind": "ExternalOutput"} if (dbg or not has_last) else {}))
    qkT_d = nc.dram_tensor("qkT", [14 * 128, L], BF16, **({"kind": "ExternalOutput"} if dbg else {}))
    Vd_d = nc.dram_tensor("Vd", [8, 128, 33, 128], BF16, **({"kind": "ExternalOutput"} if dbg else {}))
    mixT_d = nc.dram_tensor("mixT", [DM, L], BF16, **({"kind": "ExternalOutput"} if dbg else {}))
    EF_d = nc.dram_tensor("EFd", [2, 20, 2048], F32)

    sem_names = ['c_pe', 'c_act', 'c_dve', 'c_pool', 'c_sp'] + \
        ['d_%s_%d' % (q, i) for q in ('sp', 'pool') for i in range(Trk.K)]

    from contextlib import ExitStack
    with ExitStack() as top:
        sems = {n: top.enter_context(nc.semaphore(n)) for n in sem_names}
        ps = [top.enter_context(nc.psum_tensor("ps%d" % i, [128, 512], F32)) for i in range(8)]

        _uid = [0]

        def sb(stack, name, shape, dt):
            _uid[0] += 1
            return stack.enter_context(nc.sbuf_tensor("%s_%d" % (name, _uid[0]), list(shape), dt))

        ident = sb(top, "ident", [128, 128], F32)
        Jm = sb(top, "Jm", [128, 128], F32)
        permf = sb(top, "permf", [128, 128], F32)
        permb = sb(top, "permb", [128, 128], BF16)
        onesD = sb(top, "onesD", [128, 128], BF16)
        ones128 = sb(top, "ones128", [128, 128], BF16)
        ones1 = sb(top, "ones1", [128, 128], BF16)
        blk64 = sb(top, "blk64", [128, 128], BF16)
        pvec = sb(top, "pvec_sb", [128, 96], F32)
        pv2 = sb(top, "pv2", [128, 16], F32)
        tabbc = sb(top, "tabbc", [128, 640], F32)
        lamv = sb(top, "lamv", [128, 512], F32)
        esink = sb(top, "esink", [128, 32], F32)
        cst = sb(top, "cst", [128, 4], F32)
        block = top.enter_context(nc.Block())

        T = Trk(nc, sems)
        E = T.E
        V, A, G, PE = nc.vector, nc.scalar, nc.gpsimd, nc.tensor

        def body():
            T.dma('sp', ident[:], dr["cmat"].ap()[0], w=['ident'])
            T.dma('sp', Jm[:], dr["cmat"].ap()[1], w=['Jm'])
            T.dma('sp', permf[:], dr["cmat"].ap()[2], w=['permf'])
            T.dma('sp', pvec[:], pvec_d.ap()[:, :], w=['pvec'])
            T.dma('sp', tabbc[:], bass.AP(tab_d, 0, [[0, 128], [1, 640]]), w=['tabbc'])
            T.dma('sp', lamv[:], bass.AP(lam_d, 0, [[0, 128], [1, 512]]), w=['lamv'])
            T.dma('sp', esink[:], bass.AP(sinks_d, 0, [[0, 128], [1, 32]]), w=['esink'])
            T.op('dve', lambda: V.tensor_copy(out=permb[:], in_=permf[:]), r=['permf'], w=['permb'])
            T.op('pool', lambda: G.memset(onesD[:], 1.0 / 1024), w=['onesD'])
            T.op('pool', lambda: G.memset(ones128[:], 1.0 / 128), w=['ones128'])
            T.op('pool', lambda: G.memset(ones1[:], 1.0), w=['ones1'])
            T.op('pool', lambda: G.memset(blk64[:], 0.0), w=['blk64'])
            T.op('pool', lambda: G.memset(blk64[0:64, 0:64], 1.0 / 64), w=['blk64'])
            T.op('pool', lambda: G.memset(blk64[64:128, 64:128], 1.0 / 64), w=['blk64'])
            T.op('pool', lambda: G.memset(cst[:, 0:1], EPS), w=['cst'])
            T.op('act', lambda: A.activation(out=esink[:], in_=esink[:], func=AF.Exp), r=['esink'], w=['esink'])
            with ExitStack() as st:
                tab_sb = sb(st, "tab_sb", [32, 20], F32)
                oh = sb(st, "oh", [32, 2048], F32)
                ef = sb(st, "ef", [20, 2048], F32)
                efm = sb(st, "efm", [20, 2048], F32)
                msk = sb(st, "msk", [20, 2048], F32)
                T.dma('sp', tab_sb[:], tab_d.ap()[:, :], w=['tab_sb'])
                T.dma('sp', oh[:], oh_d.ap()[:, :], w=['oh'])
                T.dma('sp', msk[:], msk_d.ap()[:, :], w=['msk'])
                for c in range(4):
                    T.op('pe', lambda c=c: PE.matmul(ps[c][0:20, :], lhsT=tab_sb[:, :], rhs=oh[:, c * 512:(c + 1) * 512],
                                                     start=True, stop=True), r=['tab_sb', 'oh'], w=['ps%d' % c])
                    T.op('act', lambda c=c: A.activation(out=ef[:, c * 512:(c + 1) * 512], in_=ps[c][0:20, :], func=AF.Exp),
                         r=['ps%d' % c], w=['ef%d' % c])
                T.op('dve', lambda: V.tensor_tensor(out=efm[:], in0=ef[:], in1=msk[:], op=ALU.mult),
                     r=['ef0', 'ef1', 'ef2', 'ef3', 'msk'], w=['efm'])
                T.dma('sp', EF_d.ap()[0], ef[:], r=['ef0', 'ef1', 'ef2', 'ef3'])
                T.dma('sp', EF_d.ap()[1], efm[:], r=['efm'])
                T.barrier()

            if first:
                with ExitStack() as st:
                    xin = [sb(st, "xin%d" % i, [128, DM], F32) for i in range(2)]
                    xT = [sb(st, "xT%d" % i, [128, 8, 128], F32) for i in range(2)]
                    hT_v = hT_d.ap().rearrange("(k p) t -> p k t", p=128)
                    for ti, (t0, rows) in enumerate(KTL):
                        b = ti % 2
                        src = x_d.ap()[t0:t0 + rows, :] if t0 < S else meta_d.ap()[:, :]
                        T.dma('sp', xin[b][0:rows, :], src, w=['xin%d' % b])
                        for k in range(8):
                            pb = ps[(ti * 8 + k) % 4]
                            pn = 'ps%d' % ((ti * 8 + k) % 4)
                            T.op('pe', lambda: PE.transpose(pb[:, 0:rows], xin[b][0:rows, k * 128:(k + 1) * 128],
                                                            ident[0:rows, 0:rows]), r=['xin%d' % b, 'ident'], w=[pn])
                            if k % 2 == 0:
                                T.op('dve', lambda: V.tensor_copy(out=xT[b][:, k, 0:rows], in_=pb[:, 0:rows]),
                                     r=[pn], w=['xT%d_%d' % (b, k)])
                            else:
                                T.op('act', lambda: A.copy(out=xT[b][:, k, 0:rows], in_=pb[:, 0:rows]),
                                     r=[pn], w=['xT%d_%d' % (b, k)])
                        T.dma('sp', hT_v[:, :, t0:t0 + rows], xT[b][:, :, 0:rows], r=['xT%d_%d' % (b, k) for k in range(8)])
                    T.barrier()
            else:
                with ExitStack() as st:
                    cp = [sb(st, "cp%d" % i, [128, 8, 512], F32) for i in range(2)]
                    hT_v = hT_d.ap().rearrange("(k p) t -> p k t", p=128)
                    hin_v = hin_d.ap().rearrange("(k p) t -> p k t", p=128)
                    for ci, (t0, n) in enumerate(CH512):
                        b = ci % 2
                        T.dma('sp', cp[b][:, :, 0:n], hin_v[:, :, t0:t0 + n], w=['cp%d' % b])
                        T.dma('sp', hT_v[:, :, t0:t0 + n], cp[b][:, :, 0:n], r=['cp%d' % b])
                    T.barrier()

            for layer in layers:
                even = (layer % 2 == 0)
                li = layer // 2
                lastl = (layer == layers[-1])
                if stop == 'p0' and lastl:
                    break
                phase_a(layer, even, li)
                if stop == 'A' and lastl:
                    break
                if even:
                    phase_b_even(layer, li)
                else:
                    phase_b_odd(layer, li)
                if stop == 'B' and lastl:
                    break
                phase_c(layer, even, li, last=(layer == total_layers - 1))
            T.barrier()

        def cast(i, dst, src, srcn, dstn):
            e = ('pool', 'dve', 'act')[i % 3]
            if e == 'pool':
                T.op('pool', lambda: G.tensor_copy(out=dst, in_=src), r=[srcn], w=[dstn])
            elif e == 'dve':
                T.op('dve', lambda: V.tensor_copy(out=dst, in_=src), r=[srcn], w=[dstn])
            else:
                T.op('act', lambda: A.copy(out=dst, in_=src), r=[srcn], w=[dstn])

        def emit_norm(hc, hcn, n, gcol, hn, hnn, sq, sqn, rstd, psb, psn):
            for k in range(8):
                T.op('act' if k % 2 == 0 else 'pool',
                     (lambda k=k: A.activation(out=sq[:, k, 0:n], in_=hc[:, k, 0:n], func=AF.Square)) if k % 2 == 0 else
                     (lambda k=k: G.tensor_tensor(out=sq[:, k, 0:n], in0=hc[:, k, 0:n], in1=hc[:, k, 0:n], op=ALU.mult)),
                     r=[hcn], w=['%s_%d' % (sqn, k)])
            for k in range(8):
                T.op('pe', lambda k=k: PE.matmul(psb[:, 0:n], lhsT=onesD[:, :], rhs=sq[:, k, 0:n], start=(k == 0), stop=(k == 7)),
                     r=['%s_%d' % (sqn, k), 'onesD'], w=[psn], sig=(k == 7))
            T.op('act', lambda: A.activation(out=rstd[:, 0:n], in_=psb[:, 0:n], func=AF.Sqrt, bias=cst[:, 0:1], scale=1.0),
                 r=[psn, 'cst'], w=['rstd'])
            T.op('dve', lambda: V.reciprocal(out=rstd[:, 0:n], in_=rstd[:, 0:n]), r=['rstd'], w=['rstd'])
            for k in range(8):
                T.op('dve', lambda k=k: V.scalar_tensor_tensor(out=hn[:, k, 0:n], in0=hc[:, k, 0:n], scalar=pvec[:, gcol + k:gcol + k + 1],
                                                               in1=rstd[:, 0:n], op0=ALU.mult, op1=ALU.mult),
                     r=[hcn, 'rstd', 'pvec'], w=['%s_%d' % (hnn, k)])

        def phase_a(layer, even, li):
            NW = NW_EVEN if even else NW_ODD
            NFM = 14 if even else 10
            VOFF = NFM * 128
            with ExitStack() as st:
                win = sb(st, "win", [128, 8, NW], BF16)
                hc = [sb(st, "hcA%d" % i, [128, 8, 512], F32) for i in range(2)]
                sq = sb(st, "sqA", [128, 8, 512], BF16)
                hn = sb(st, "hnA", [128, 8, 512], BF16)
                rstd = sb(st, "rstdA", [128, 512], F32)
                stg = [sb(st, "stgA%d" % i, [128, 512], BF16) for i in range(4)]
                stgv = [sb(st, "stgvA%d" % i, [128, 1024], BF16) for i in range(2)]
                if even:
                    cosb = sb(st, "cosb", [128, 512], F32)
                    sinb = sb(st, "sinb", [128, 512], F32)
                    sqb = sb(st, "sqb", [128, 512], BF16)
                    rsb = sb(st, "rsb", [128, 512], F32)
                    xn = sb(st, "xn", [128, 512], F32)
                    xnb = sb(st, "xnb", [128, 512], BF16)
                    t1 = sb(st, "t1", [128, 512], F32)
                    t2 = sb(st, "t2", [128, 512], F32)
                wsrc = dr["w_in%d" % layer].ap()
                wstg = [sb(st, "wstgA%d" % i, [128, NW], F32) for i in range(2)]
                for k in range(8):
                    T.dma('sp', wstg[k % 2][:, :], wsrc[k * 128:(k + 1) * 128, :], w=['wstgA%d' % (k % 2)])
                    cast(k, win[:, k, :], wstg[k % 2][:, :], 'wstgA%d' % (k % 2), 'win_%d' % k)
                voff = 512 if even else 0
                for b in range(2):
                    T.op('pool', lambda b=b: G.memset(stgv[b][:, voff:voff + 512], 1.0), w=['stgv%d' % b])
                if even:
                    T.op('dve', lambda: V.tensor_scalar(out=pv2[:, 0:1], in0=pvec[:, 74 + 2 * li:75 + 2 * li], scalar1=0.125,
                                                        scalar2=None, op0=ALU.mult), r=['pvec'], w=['pv2'])
                hT_v = hT_d.ap().rearrange("(k p) t -> p k t", p=128)
                stg_i = 0
                for ci, (t0, n) in enumerate(CH512):
                    b = ci % 2
                    hcn = 'hcA%d' % b
                    T.dma('sp', hc[b][:, :, 0:n], hT_v[:, :, t0:t0 + n], w=[hcn])
                    if even:
                        T.dma('sp', cosb[:, 0:n], rope_d.ap()[0][:, t0:t0 + n], w=['cosb'])
                        T.dma('sp', sinb[:, 0:n], rope_d.ap()[1][:, t0:t0 + n], w=['sinb'])
                    emit_norm(hc[b], hcn, n, layer * 8, hn, 'hnA', sq, 'sqA', rstd, ps[0], 'ps0')
                    hn_r = ['hnA_%d' % k for k in range(8)]
                    for fo in range(NFM if not os.environ.get('SKIP_FM') else 0):
                        pi = 1 + (fo % 3)
                        pb, pn = ps[pi], 'ps%d' % pi
                        for k in range(8):
                            T.op('pe', lambda k=k: PE.matmul(pb[:, 0:n], lhsT=win[:, k, fo * 128:(fo + 1) * 128], rhs=hn[:, k, 0:n],
                                                             start=(k == 0), stop=(k == 7)),
                                 r=hn_r + ['win_%d' % k], w=[pn], sig=(k == 7))
                        sg = stg[stg_i % 4]
                        sgn = 'stgA%d' % (stg_i % 4)
                        stg_i += 1
                        if even and 8 <= fo < 14 and not os.environ.get('SKIP_B'):
                            isq = fo < 12
                            T.op('act', lambda: A.activation(out=sqb[:, 0:n], in_=pb[:, 0:n], func=AF.Square), r=[pn], w=['sqb'])
                            T.op('pe', lambda: PE.matmul(ps[4][:, 0:n], lhsT=blk64[:, :], rhs=sqb[:, 0:n], start=True, stop=True),
                                 r=['sqb', 'blk64'], w=['ps4'])
                            T.op('act', lambda: A.activation(out=rsb[:, 0:n], in_=ps[4][:, 0:n], func=AF.Sqrt, bias=cst[:, 0:1], scale=1.0),
                                 r=['ps4', 'cst'], w=['rsb'])
                            T.op('dve', lambda: V.reciprocal(out=rsb[:, 0:n], in_=rsb[:, 0:n]), r=['rsb'], w=['rsb'])
                            gap = pv2[:, 0:1] if isq else pvec[:, 75 + 2 * li:76 + 2 * li]
                            T.op('dve', lambda: V.scalar_tensor_tensor(out=xn[:, 0:n], in0=pb[:, 0:n], scalar=gap, in1=rsb[:, 0:n],
                                                                       op0=ALU.mult, op1=ALU.mult),
                                 r=[pn, 'rsb', 'pv2', 'pvec'], w=['xn'])
                            T.op('act', lambda: A.copy(out=xnb[:, 0:n], in_=xn[:, 0:n]), r=['xn'], w=['xnb'])
                            T.op('pe', lambda: PE.matmul(ps[5][:, 0:n], lhsT=permb[:, :], rhs=xnb[:, 0:n], start=True, stop=True),
                                 r=['xnb', 'permb'], w=['ps5'])
                            T.op('pool', lambda: G.tensor_tensor(out=t1[:, 0:n], in0=xn[:, 0:n], in1=cosb[:, 0:n], op=ALU.mult),
                                 r=['xn', 'cosb'], w=['t1'])
                            T.op('dve', lambda: V.tensor_tensor(out=t2[:, 0:n], in0=ps[5][:, 0:n], in1=sinb[:, 0:n], op=ALU.mult),
                                 r=['ps5', 'sinb'], w=['t2'])
                            T.op('pool', lambda: G.tensor_tensor(out=sg[:, 0:n], in0=t1[:, 0:n], in1=t2[:, 0:n], op=ALU.add),
                                 r=['t1', 't2'], w=[sgn])
                        else:
                            isq = (fo < 4) if even else (fo < 8)
                            if isq:
                                T.op('act', lambda: A.activation(out=sg[:, 0:n], in_=pb[:, 0:n], func=AF.Copy, scale=0.125), r=[pn], w=[sgn])
                            else:
                                T.op('dve', lambda: V.tensor_copy(out=sg[:, 0:n], in_=pb[:, 0:n]), r=[pn], w=[sgn])
                        T.dma('sp', qkT_d.ap()[fo * 128:(fo + 1) * 128, t0:t0 + n], sg[:, 0:n], r=[sgn])
                    ntile = max(1, n // 128)
                    for tt in range(ntile if not os.environ.get('SKIP_V') else 0):
                        rows = min(128, n)
                        tile_idx = (t0 // 128) + tt
                        sv = stgv[tile_idx % 2]
                        svn = 'stgv%d' % (tile_idx % 2)
                        tsl = slice(tt * 128, tt * 128 + rows)
                        if even:
                            for k in range(8):
                                T.op('pe', lambda k=k: PE.matmul(ps[6][0:rows, :], lhsT=hn[:, k, tsl], rhs=win[:, k, VOFF:VOFF + 512],
                                                                 start=(k == 0), stop=(k == 7)), r=hn_r + ['win_%d' % k], w=['ps6'], sig=(k == 7))
                            T.op('act', lambda: A.copy(out=sv[0:rows, 0:512], in_=ps[6][0:rows, :]), r=['ps6'], w=[svn + 'a'])
                            vb0 = VOFF + 512
                        else:
                            vb0 = VOFF
                        for k in range(8):
                            T.op('pe', lambda k=k: PE.matmul(ps[7][0:rows, 0:128], lhsT=hn[:, k, tsl], rhs=win[:, k, vb0:vb0 + 128],
                                                             start=(k == 0), stop=(k == 7)), r=hn_r + ['win_%d' % k], w=['ps7'], sig=(k == 7))
                        for j in range(2):
                            T.op('dve', lambda j=j: V.tensor_copy(out=sv[0:rows, voff + j * 128:voff + j * 128 + 64],
                                                                  in_=ps[7][0:rows, j * 64:(j + 1) * 64]), r=['ps7', svn], w=[svn + 'b%d' % j])
                            T.op('dve', lambda j=j: V.tensor_copy(out=sv[0:rows, voff + 256 + j * 128 + 64:voff + 256 + (j + 1) * 128],
                                                                  in_=ps[7][0:rows, j * 64:(j + 1) * 64]), r=['ps7', svn], w=[svn + 'c%d' % j])
                        ns = 8 if even else 4
                        for s_ in range(ns):
                            T.dma('sp', Vd_d.ap()[s_, 0:rows, tile_idx, :], sv[0:rows, s_ * 128:(s_ + 1) * 128],
                                  r=[svn + 'a', svn + 'b0', svn + 'b1', svn + 'c0', svn + 'c1', svn])
                T.barrier()

        def flip(dst_ap, hk, m0, nk, nq, psb, psn, dstn, use_act=False):
            T.op('pe', lambda: PE.matmul(psb[0:nk, 0:nq], lhsT=hk[:, m0:m0 + nk], rhs=Jm[:, 0:nq], start=True, stop=True),
                 r=['hk', 'Jm'], w=[psn])
            T.op('dve', lambda: V.tensor_copy(out=dst_ap, in_=psb[0:nk, 0:nq]), r=[psn], w=[dstn])

        def bucket_const(kmin, kmax, qmin, qmax):
            rel = np.arange(kmin - qmax, kmax - qmin + 1)
            b = rel_bucket_np(rel)
            if np.all(b == b[0]):
                return int(b[0])
            return None

        def phase_b_even(layer, li):
            lam_init = 0.8 - 0.6 * math.exp(-0.3 * layer)
            with ExitStack() as st:
                QT = sb(st, "QT", [128, L], BF16)
                KT = sb(st, "KT", [128, L], BF16)
                Vt = sb(st, "Vt", [128, 33, 128], BF16)
                Vt2 = sb(st, "Vt2", [128, 33, 128], BF16)
                hk = sb(st, "hk", [128, 1152], F32)
                ETr = sb(st, "ETr", [128, 6, 512], F32)
                ETmk = sb(st, "ETmk", [16, 512], F32)
                ETq0 = sb(st, "ETq0", [128, 16], F32)
                ETmm = sb(st, "ETmm", [16, 16], F32)
                PT = [sb(st, "PT%d" % i, [128, 512], BF16) for i in range(3)]
                PTf = [sb(st, "PTf%d" % i, [128, 512], F32) for i in range(2)]
                rc = [sb(st, "rc%d" % i, [128, 512], F32) for i in range(2)]
                oo = [sb(st, "oo%d" % i, [128, 512], F32) for i in range(2)]
                dd = sb(st, "dd", [128, 512], F32)
                sqd = sb(st, "sqd", [128, 512], BF16)
                rsd = sb(st, "rsd", [128, 512], F32)
                stg = [sb(st, "stgB%d" % i, [128, 512], BF16) for i in range(2)]
                tmp = sb(st, "tmpl", [128, 256], F32)
                lo = li * 256
                T.op('dve', lambda: V.tensor_tensor(out=tmp[:, 0:64], in0=lamv[:, lo:lo + 64], in1=lamv[:, lo + 64:lo + 128], op=ALU.mult),
                     r=['lamv'], w=['tmpl'])
                T.op('dve', lambda: V.tensor_tensor(out=tmp[:, 64:128], in0=lamv[:, lo + 128:lo + 192], in1=lamv[:, lo + 192:lo + 256], op=ALU.mult),
                     r=['lamv', 'tmpl'], w=['tmpl'])
                T.op('dve', lambda: V.reduce_sum(out=tmp[:, 128:129], in_=tmp[:, 0:64], axis=mybir.AxisListType.X), r=['tmpl'], w=['tmpl'])
                T.op('dve', lambda: V.reduce_sum(out=tmp[:, 129:130], in_=tmp[:, 64:128], axis=mybir.AxisListType.X), r=['tmpl'], w=['tmpl'])
                T.op('act', lambda: A.activation(out=tmp[:, 130:132], in_=tmp[:, 128:130], func=AF.Exp), r=['tmpl'], w=['tmpl'])
                T.op('dve', lambda: V.tensor_tensor(out=tmp[:, 132:133], in0=tmp[:, 131:132], in1=tmp[:, 130:131], op=ALU.subtract),
                     r=['tmpl'], w=['tmpl'])
                T.op('dve', lambda: V.tensor_scalar(out=pv2[:, 1:2], in0=tmp[:, 132:133], scalar1=-lam_init, scalar2=None, op0=ALU.add),
                     r=['tmpl'], w=['pv2'])
                T.op('dve', lambda: V.tensor_scalar(out=pv2[:, 2:3], in0=pvec[:, 72 + li:73 + li], scalar1=(1.0 - lam_init), scalar2=None,
                                                    op0=ALU.mult), r=['pvec', 'pv2'], w=['pv2'])
                for h in range(4):
                    T.dma('sp', QT[:, :], qkT_d.ap()[h * 128:(h + 1) * 128, :], w=['QT'])
                    T.dma('sp', KT[:, :], qkT_d.ap()[(4 + h) * 128:(5 + h) * 128, :], w=['KT'])
                    T.dma('sp', Vt[:, :, :], Vd_d.ap()[h], w=['Vt'])
                    T.dma('sp', hk[:, :], bass.AP(EF_d, h * 2048 + RB - 512 - 127, [[1, 128], [1, 1152]]), w=['hk'])
                    fi = 0
                    for dc in range(6):
                        for s_ in range(4):
                            Dv = 128 * (dc - 1 - s_)
                            flip(ETr[:, dc, s_ * 128:(s_ + 1) * 128], hk, Dv + 512, 128, 128, ps[7], 'ps7', 'ETr')
                    for s_ in range(4):
                        flip(ETmk[:, s_ * 128:(s_ + 1) * 128], hk, -16 - 128 * s_ + 512, 16, 128, ps[7], 'ps7', 'ETmk')
                    flip(ETq0[:, :], hk, 16 + 512, 128, 16, ps[7], 'ps7', 'ETq0')
                    flip(ETmm[:, :], hk, 0 + 512, 16, 16, ps[7], 'ps7', 'ETmm')
                    ei = 0
                    for j, (q0, n) in enumerate(CH512):
                        qmin = tok_pos(q0)
                        qmax = tok_pos(q0 + n - 1)
                        for c in range(2):
                            Ob, On = ps[3 + c], 'ps%d' % (3 + c)
                            Rb, Rn = ps[5 + c], 'ps%d' % (5 + c)
                            hs = slice(64 * c, 64 * c + 64)
                            for t, (k0, kr) in enumerate(KTL):
                                Sb, Sn = ps[ei % 3], 'ps%d' % (ei % 3)
                                P_, Pn = PT[ei % 3], 'PT%d' % (ei % 3)
                                ei += 1
                                T.op('pe', lambda: PE.matmul(Sb[0:kr, 0:n], lhsT=KT[hs, k0:k0 + kr], rhs=QT[hs, q0:q0 + n], start=True, stop=True),
                                     r=['KT', 'QT'], w=[Sn])
                                bc = bucket_const(tok_pos(k0), tok_pos(k0 + kr - 1), qmin, qmax)
                                if bc is not None:
                                    col = bc * 20 + h
                                    T.op('act', lambda: A.activation(out=P_[0:kr, 0:n], in_=Sb[0:kr, 0:n], func=AF.Exp,
                                                                     bias=tabbc[0:kr, col:col + 1], scale=1.0), r=[Sn, 'tabbc'], w=[Pn])
                                else:
                                    if t < 32 and j < 8:
                                        et = ETr[:, t - 4 * j + 1, :]
                                        etn = 'ETr'
                                    elif t == 32 and j < 8:
                                        et = ETmk[:, :]
                                        etn = 'ETmk'
                                    elif t < 32:
                                        et = ETq0[:, :]
                                        etn = 'ETq0'
                                    else:
                                        et = ETmm[:, :]
                                        etn = 'ETmm'
                                    pf, pfn = PTf[ei % 2], 'PTf%d' % (ei % 2)
                                    T.op('act', lambda: A.activation(out=pf[0:kr, 0:n], in_=Sb[0:kr, 0:n], func=AF.Exp), r=[Sn], w=[pfn])
                                    T.op('dve', lambda: V.tensor_tensor(out=P_[0:kr, 0:n], in0=pf[0:kr, 0:n], in1=et, op=ALU.mult),
                                         r=[pfn, etn], w=[Pn])
                                T.op('pe', lambda: PE.matmul(Ob[:, 0:n], lhsT=Vt[0:kr, t, :], rhs=P_[0:kr, 0:n], start=(t == 0), stop=(t == 32)),
                                     r=[Pn, 'Vt'], w=[On], sig=(t == 32))
                                T.op('pe', lambda: PE.matmul(Rb[:, 0:n], lhsT=ones1[0:kr, :], rhs=P_[0:kr, 0:n], start=(t == 0), stop=(t == 32)),
                                     r=[Pn, 'ones1'], w=[Rn], sig=True)
                            T.op('dve', lambda: V.reciprocal(out=rc[c][:, 0:n], in_=Rb[:, 0:n]), r=[Rn], w=['rc%d' % c])
                            T.op('dve', lambda: V.tensor_tensor(out=oo[c][:, 0:n], in0=Ob[:, 0:n], in1=rc[c][:, 0:n], op=ALU.mult),
                                 r=[On, 'rc%d' % c], w=['oo%d' % c])
                        T.op('dve', lambda: V.scalar_tensor_tensor(out=dd[:, 0:n], in0=oo[1][:, 0:n], scalar=pv2[:, 1:2], in1=oo[0][:, 0:n],
                                                                   op0=ALU.mult, op1=ALU.add), r=['oo0', 'oo1', 'pv2'], w=['dd'])
                        T.op('act', lambda: A.activation(out=sqd[:, 0:n], in_=dd[:, 0:n], func=AF.Square), r=['dd'], w=['sqd'])
                        T.op('pe', lambda: PE.matmul(ps[7][:, 0:n], lhsT=ones128[:, :], rhs=sqd[:, 0:n], start=True, stop=True),
                             r=['sqd', 'ones128'], w=['ps7'])
                        T.op('act', lambda: A.activation(out=rsd[:, 0:n], in_=ps[7][:, 0:n], func=AF.Sqrt, bias=cst[:, 0:1], scale=1.0),
                             r=['ps7', 'cst'], w=['rsd'])
                        T.op('dve', lambda: V.reciprocal(out=rsd[:, 0:n], in_=rsd[:, 0:n]), r=['rsd'], w=['rsd'])
                        sg, sgn = stg[j % 2], 'stgB%d' % (j % 2)
                        T.op('dve', lambda: V.scalar_tensor_tensor(out=sg[:, 0:n], in0=dd[:, 0:n], scalar=pv2[:, 2:3], in1=rsd[:, 0:n],
                                                                   op0=ALU.mult, op1=ALU.mult), r=['dd', 'rsd', 'pv2'], w=[sgn])
                        T.dma('sp', mixT_d.ap()[h * 128:(h + 1) * 128, q0:q0 + n], sg[:, 0:n], r=[sgn])
                ei = 0
                for jg in range(2):
                    T.dma('sp', KT[:, :], qkT_d.ap()[(12 + jg) * 128:(13 + jg) * 128, :], w=['KT'])
                    T.dma('sp', Vt[:, :, :], Vd_d.ap()[4 + jg], w=['Vt'])
                    T.dma('sp', Vt2[:, :, :], Vd_d.ap()[6 + jg], w=['Vt2'])
                    for qc in range(2):
                        ch = jg * 2 + qc
                        T.dma('sp', QT[:, :], qkT_d.ap()[(8 + ch) * 128:(9 + ch) * 128, :], w=['QT'])
                        for j, (q0, n) in enumerate(CH512):
                            sg, sgn = stg[j % 2], 'stgB%d' % (j % 2)
                            for p in range(2):
                                Ob, On = ps[3 + p], 'ps%d' % (3 + p)
                                hs = slice(64 * p, 64 * p + 64)
                                vt, vtn = (Vt, 'Vt') if p == 0 else (Vt2, 'Vt2')
                                for t, (k0, kr) in enumerate(KTL):
                                    Sb, Sn = ps[ei % 3], 'ps%d' % (ei % 3)
                                    P_, Pn = PT[ei % 3], 'PT%d' % (ei % 3)
                                    ei += 1
                                    T.op('pe', lambda: PE.matmul(Sb[0:kr, 0:n], lhsT=KT[hs, k0:k0 + kr], rhs=QT[hs, q0:q0 + n], start=True, stop=True),
                                         r=['KT', 'QT'], w=[Sn])
                                    T.op('act', lambda: A.activation(out=P_[0:kr, 0:n], in_=Sb[0:kr, 0:n], func=AF.Exp), r=[Sn], w=[Pn])
                                    T.op('pe', lambda: PE.matmul(Ob[:, 0:n], lhsT=vt[0:kr, t, :], rhs=P_[0:kr, 0:n], start=(t == 0), stop=(t == 32)),
                                         r=[Pn, vtn], w=[On], sig=True)
                                os_ = slice(64 * p, 64 * p + 64)
                                rs_ = slice(64 * (1 - p), 64 * (1 - p) + 64)
                                T.op('dve', lambda: V.tensor_copy(out=rc[p][os_, 0:n], in_=Ob[rs_, 0:n]), r=[On], w=['rc%d' % p])
                                T.op('dve', lambda: V.reciprocal(out=rc[p][os_, 0:n], in_=rc[p][os_, 0:n]), r=['rc%d' % p], w=['rc%d' % p])
                                T.op('dve', lambda: V.tensor_tensor(out=sg[os_, 0:n], in0=Ob[os_, 0:n], in1=rc[p][os_, 0:n], op=ALU.mult),
                                     r=[On, 'rc%d' % p, sgn], w=[sgn + '_%d' % p])
                            T.dma('sp', mixT_d.ap()[512 + ch * 128:512 + (ch + 1) * 128, q0:q0 + n], sg[:, 0:n], r=[sgn + '_0', sgn + '_1'], w=[sgn])
                T.barrier()

        def phase_b_odd(layer, li):
            with ExitStack() as st:
                QT = sb(st, "QT", [128, L], BF16)
                KT = sb(st, "KT", [128, L], BF16)
                Vt = sb(st, "Vt", [128, 33, 128], BF16)
                Vt2 = sb(st, "Vt2", [128, 33, 128], BF16)
                hkm = sb(st, "hkm", [128, 384], F32)
                hku = sb(st, "hku", [128, 144], F32)
                ETo = [sb(st, "ETo%d" % p, [128, 384], F32) for p in range(2)]
                ETk0 = [sb(st, "ETk0%d" % p, [16, 128], F32) for p in range(2)]
                ETq0 = [sb(st, "ETq0%d" % p, [128, 16], F32) for p in range(2)]
                ETmm = [sb(st, "ETmm%d" % p, [16, 16], F32) for p in range(2)]
                PT = [sb(st, "PT%d" % i, [128, 384], BF16) for i in range(3)]
                PTf = [sb(st, "PTf%d" % i, [128, 384], F32) for i in range(2)]
                PTm = [sb(st, "PTm%d" % i, [16, 512], BF16) for i in range(2)]
                PTmf = sb(st, "PTmf", [16, 512], F32)
                rc = [sb(st, "rc%d" % i, [128, 512], F32) for i in range(2)]
                stg = [sb(st, "stgB%d" % i, [128, 512], BF16) for i in range(2)]
                ei = 0
                mi = 0
                for ch in range(8):
                    jg = ch // 4
                    if ch % 4 == 0:
                        T.dma('sp', KT[:, :], qkT_d.ap()[(8 + jg) * 128:(9 + jg) * 128, :], w=['KT'])
                        T.dma('sp', Vt[:, :, :], Vd_d.ap()[jg], w=['Vt'])
                        T.dma('sp', Vt2[:, :, :], Vd_d.ap()[2 + jg], w=['Vt2'])
                    T.dma('sp', QT[:, :], qkT_d.ap()[ch * 128:(ch + 1) * 128, :], w=['QT'])
                    for p in range(2):
                        hh = 4 + ch * 2 + p
                        T.dma('sp', hkm[:, :], bass.AP(EF_d, (20 + hh) * 2048 + RB - 128 - 127, [[1, 128], [1, 384]]), w=['hk'])
                        T.dma('sp', hku[:, :], bass.AP(EF_d, hh * 2048 + RB - 16 - 127, [[1, 128], [1, 144]]), w=['hku'])
                        for s_ in range(3):
                            flip(ETo[p][:, s_ * 128:(s_ + 1) * 128], hkm, 128 * (s_ - 1) + 128, 128, 128, ps[7], 'ps7', 'ETo%d' % p)
                        flip(ETq0[p][:, :], hkm, 16 + 128, 128, 16, ps[7], 'ps7', 'ETq0%d' % p)
                        T.op('pe', lambda: PE.matmul(ps[7][0:16, 0:128], lhsT=hku[:, 0:16], rhs=Jm[:, 0:128], start=True, stop=True),
                             r=['hku', 'Jm'], w=['ps7'])
                        T.op('dve', lambda: V.tensor_copy(out=ETk0[p][:, :], in_=ps[7][0:16, 0:128]), r=['ps7'], w=['ETk0%d' % p])
                        T.op('pe', lambda: PE.matmul(ps[7][0:16, 0:16], lhsT=hku[:, 16:32], rhs=Jm[:, 0:16], start=True, stop=True),
                             r=['hku', 'Jm'], w=['ps7'])
                        T.op('dve', lambda: V.tensor_copy(out=ETmm[p][:, :], in_=ps[7][0:16, 0:16]), r=['ps7'], w=['ETmm%d' % p])
                    for j, (q0, n) in enumerate(CH512):
                        sg, sgn = stg[j % 2], 'stgB%d' % (j % 2)
                        for p in range(2):
                            hh = 4 + ch * 2 + p
                            hs = slice(64 * p, 64 * p + 64)
                            vt, vtn = (Vt, 'Vt') if p == 0 else (Vt2, 'Vt2')
                            Ob, On = ps[3 + p], 'ps%d' % (3 + p)
                            pm, pmn = PTm[mi % 2], 'PTm%d' % (mi % 2)
                            mi += 1
                            T.op('pe', lambda: PE.matmul(ps[5][0:16, 0:n], lhsT=KT[hs, S:S + 16], rhs=QT[hs, q0:q0 + n], start=True, stop=True),
                                 r=['KT', 'QT'], w=['ps5'])
                            if j == 0:
                                T.op('act', lambda: A.activation(out=PTmf[:, 0:n], in_=ps[5][0:16, 0:n], func=AF.Exp,
                                                                 bias=tabbc[0:16, 15 * 20 + hh:15 * 20 + hh + 1], scale=1.0),
                                     r=['ps5', 'tabbc'], w=['PTmf'])
                                T.op('act', lambda: A.activation(out=PTmf[:, 0:128], in_=ps[5][0:16, 0:128], func=AF.Exp),
                                     r=['ps5', 'PTmf'], w=['PTmf'])
                                T.op('dve', lambda: V.tensor_tensor(out=PTmf[:, 0:128], in0=PTmf[:, 0:128], in1=ETk0[p][:, :], op=ALU.mult),
                                     r=['PTmf', 'ETk0%d' % p], w=['PTmf'])
                                T.op('dve', lambda: V.tensor_copy(out=pm[:, 0:n], in_=PTmf[:, 0:n]), r=['PTmf'], w=[pmn])
                            elif j < 8:
                                T.op('act', lambda: A.activation(out=pm[:, 0:n], in_=ps[5][0:16, 0:n], func=AF.Exp,
                                                                 bias=tabbc[0:16, 15 * 20 + hh:15 * 20 + hh + 1], scale=1.0),
                                     r=['ps5', 'tabbc'], w=[pmn])
                            else:
                                T.op('act', lambda: A.activation(out=PTmf[:, 0:16], in_=ps[5][0:16, 0:16], func=AF.Exp), r=['ps5'], w=['PTmf'])
                                T.op('dve', lambda: V.tensor_tensor(out=pm[:, 0:16], in0=PTmf[:, 0:16], in1=ETmm[p][:, :], op=ALU.mult),
                                     r=['PTmf', 'ETmm%d' % p], w=[pmn])
                            if j < 8:
                                for ul in range(4):
                                    u = j * 4 + ul
                                    qs = slice(u * 128, u * 128 + 128)
                                    tl = [t for t in (u - 1, u, u + 1) if 0 <= t < 32]
                                    c0 = (tl[0] - (u - 1)) * 128
                                    w_ = len(tl) * 128
                                    Sb, Sn = ps[ei % 3], 'ps%d' % (ei % 3)
                                    P_, Pn = PT[ei % 3], 'PT%d' % (ei % 3)
                                    pf, pfn = PTf[ei % 2], 'PTf%d' % (ei % 2)
                                    ei += 1
                                    for si, t in enumerate(tl):
                                        T.op('pe', lambda: PE.matmul(Sb[:, si * 128:(si + 1) * 128], lhsT=KT[hs, t * 128:(t + 1) * 128], rhs=QT[hs, qs],
                                                                     start=True, stop=True), r=['KT', 'QT'], w=[Sn], sig=(si == len(tl) - 1))
                                    T.op('act', lambda: A.activation(out=pf[:, 0:w_], in_=Sb[:, 0:w_], func=AF.Exp), r=[Sn], w=[pfn])
                                    T.op('dve', lambda: V.tensor_tensor(out=P_[:, 0:w_], in0=pf[:, 0:w_], in1=ETo[p][:, c0:c0 + w_], op=ALU.mult),
                                         r=[pfn, 'ETo%d' % p], w=[Pn])
                                    osl = slice(ul * 128, ul * 128 + 128)
                                    T.op('pe', lambda: PE.matmul(Ob[:, osl], lhsT=vt[0:16, 32, :], rhs=pm[:, osl], start=True, stop=False),
                                         r=[pmn, vtn], w=[On], sig=False)
                                    for si, t in enumerate(tl):
                                        last = (si == len(tl) - 1)
                                        T.op('pe', lambda: PE.matmul(Ob[:, osl], lhsT=vt[:, t, :], rhs=P_[:, si * 128:(si + 1) * 128], start=False, stop=last),
                                             r=[Pn, vtn], w=[On], sig=last)
                            else:
                                Sb, Sn = ps[ei % 3], 'ps%d' % (ei % 3)
                                P_, Pn = PT[ei % 3], 'PT%d' % (ei % 3)
                                pf, pfn = PTf[ei % 2], 'PTf%d' % (ei % 2)
                                ei += 1
                                T.op('pe', lambda: PE.matmul(Sb[:, 0:16], lhsT=KT[hs, 0:128], rhs=QT[hs, S:S + 16], start=True, stop=True),
                                     r=['KT', 'QT'], w=[Sn])
                                T.op('act', lambda: A.activation(out=pf[:, 0:16], in_=Sb[:, 0:16], func=AF.Exp), r=[Sn], w=[pfn])
                                T.op('dve', lambda: V.tensor_tensor(out=P_[:, 0:16], in0=pf[:, 0:16], in1=ETq0[p][:, :], op=ALU.mult),
                                     r=[pfn, 'ETq0%d' % p], w=[Pn])
                                T.op('pe', lambda: PE.matmul(Ob[:, 0:16], lhsT=vt[0:16, 32, :], rhs=pm[:, 0:16], start=True, stop=False),
                                     r=[pmn, vtn], w=[On], sig=False)
                                T.op('pe', lambda: PE.matmul(Ob[:, 0:16], lhsT=vt[:, 0, :], rhs=P_[:, 0:16], start=False, stop=True),
                                     r=[Pn, vtn], w=[On], sig=True)
                            os_ = slice(64 * p, 64 * p + 64)
                            rs_ = slice(64 * (1 - p), 64 * (1 - p) + 64)
                            sc = li * 16 + ch * 2 + p
                            T.op('dve', lambda: V.tensor_copy(out=rc[p][os_, 0:n], in_=Ob[rs_, 0:n]), r=[On], w=['rc%d' % p])
                            T.op('dve', lambda: V.tensor_scalar(out=rc[p][os_, 0:n], in0=rc[p][os_, 0:n], scalar1=esink[os_, sc:sc + 1], scalar2=None,
                                                                op0=ALU.add), r=['rc%d' % p, 'esink'], w=['rc%d' % p])
                            T.op('dve', lambda: V.reciprocal(out=rc[p][os_, 0:n], in_=rc[p][os_, 0:n]), r=['rc%d' % p], w=['rc%d' % p])
                            T.op('dve', lambda: V.tensor_tensor(out=sg[os_, 0:n], in0=Ob[os_, 0:n], in1=rc[p][os_, 0:n], op=ALU.mult),
                                 r=[On, 'rc%d' % p, sgn], w=[sgn + '_%d' % p])
                        T.dma('sp', mixT_d.ap()[ch * 128:(ch + 1) * 128, q0:q0 + n], sg[:, 0:n], r=[sgn + '_0', sgn + '_1'], w=[sgn])
                T.barrier()

        def phase_c(layer, even, li, last):
            with ExitStack() as st:
                wout = sb(st, "wout", [128, 8, DM], BF16)
                wup = sb(st, "wup", [128, 8, DFF], BF16)
                wdown = sb(st, "wdown", [128, 32, DM], BF16)
                wo_src = dr["w_out%d" % layer].ap()
                wstg = [sb(st, "wstgC%d" % i, [128, 1024], F32) for i in range(2)]
                pc = 0
                for k in range(8):
                    sn = 'wstgC%d' % (pc % 2)
                    T.dma('sp', wstg[pc % 2][:, :], wo_src[k * 128:(k + 1) * 128, :], w=[sn])
                    cast(pc, wout[:, k, :], wstg[pc % 2][:, :], sn, 'wout_%d' % k)
                    pc += 1
                for k in range(8):
                    for q4 in range(4):
                        sn = 'wstgC%d' % (pc % 2)
                        T.dma('sp', wstg[pc % 2][:, :], dr["w_up%d" % layer].ap()[k * 128:(k + 1) * 128, q4 * 1024:(q4 + 1) * 1024], w=[sn])
                        cast(pc, wup[:, k, q4 * 1024:(q4 + 1) * 1024], wstg[pc % 2][:, :], sn, 'wup_%d_%d' % (k, q4))
                        pc += 1
                for f in range(32):
                    sn = 'wstgC%d' % (pc % 2)
                    T.dma('sp', wstg[pc % 2][:, :], dr["w_down%d" % layer].ap()[f * 128:(f + 1) * 128, :], w=[sn])
                    cast(pc, wdown[:, f, :], wstg[pc % 2][:, :], sn, 'wdown_%d' % f)
                    pc += 1
                NB = 1 if last else 2
                hc = [sb(st, "hcC%d" % i, [128, 8, 256], F32) for i in range(NB)]
                mx = [sb(st, "mxC%d" % i, [128, 8, 256], BF16) for i in range(NB)]
                hn = sb(st, "hnC", [128, 8, 256], BF16)
                u = sb(st, "uC", [128, 32, 256], BF16)
                rstd = sb(st, "rstdC", [128, 256], F32)
                rl = [sb(st, "rlC%d" % i, [128, 256], F32) for i in range(2)]
                if last:
                    yo = sb(st, "yo", [128, 8, 256], F32)
                    ot = [sb(st, "ot%d" % i, [128, DM], F32) for i in range(1)]
                hT_v = hT_d.ap().rearrange("(k p) t -> p k t", p=128)
                mT_v = mixT_d.ap().rearrange("(k p) t -> p k t", p=128)
                pi = 0
                for ci, (t0, n) in enumerate(CH256):
                    b = ci % NB
                    hcn, mxn = 'hcC%d' % b, 'mxC%d' % b
                    T.dma('sp', hc[b][:, :, 0:n], hT_v[:, :, t0:t0 + n], w=[hcn + '_%d' % k for k in range(8)] + [hcn])
                    T.dma('sp', mx[b][:, :, 0:n], mT_v[:, :, t0:t0 + n], w=[mxn])
                    for dc in range(8):
                        pb, pn = ps[1 + pi % 6], 'ps%d' % (1 + pi % 6)
                        pi += 1
                        for k in range(8):
                            T.op('pe', lambda k=k: PE.matmul(pb[:, 0:n], lhsT=wout[:, k, dc * 128:(dc + 1) * 128], rhs=mx[b][:, k, 0:n],
                                                             start=(k == 0), stop=(k == 7)), r=[mxn, 'wout_%d' % k], w=[pn], sig=(k == 7))
                        T.op('dve', lambda: V.tensor_tensor(out=hc[b][:, dc, 0:n], in0=pb[:, 0:n], in1=hc[b][:, dc, 0:n], op=ALU.add),
                             r=[pn, hcn + '_%d' % dc], w=[hcn + '_%d' % dc])
                    hk_all = [hcn + '_%d' % k for k in range(8)]
                    for k in range(8):
                        T.op('act' if k % 2 == 0 else 'pool',
                             (lambda k=k: A.activation(out=u[:, k, 0:n], in_=hc[b][:, k, 0:n], func=AF.Square)) if k % 2 == 0 else
                             (lambda k=k: G.tensor_tensor(out=u[:, k, 0:n], in0=hc[b][:, k, 0:n], in1=hc[b][:, k, 0:n], op=ALU.mult)),
                             r=[hcn + '_%d' % k], w=['uC_%d' % k])
                    for k in range(8):
                        T.op('pe', lambda k=k: PE.matmul(ps[0][:, 0:n], lhsT=onesD[:, :], rhs=u[:, k, 0:n], start=(k == 0), stop=(k == 7)),
                             r=['uC_%d' % k, 'onesD'], w=['ps0'], sig=(k == 7))
                    T.op('act', lambda: A.activation(out=rstd[:, 0:n], in_=ps[0][:, 0:n], func=AF.Sqrt, bias=cst[:, 0:1], scale=1.0),
                         r=['ps0', 'cst'], w=['rstd'])
                    T.op('dve', lambda: V.reciprocal(out=rstd[:, 0:n], in_=rstd[:, 0:n]), r=['rstd'], w=['rstd'])
                    gcol = 32 + layer * 8
                    for k in range(8):
                        T.op('dve', lambda k=k: V.scalar_tensor_tensor(out=hn[:, k, 0:n], in0=hc[b][:, k, 0:n], scalar=pvec[:, gcol + k:gcol + k + 1],
                                                                       in1=rstd[:, 0:n], op0=ALU.mult, op1=ALU.mult),
                             r=[hcn + '_%d' % k, 'rstd', 'pvec'], w=['hnC_%d' % k])
                    hn_r = ['hnC_%d' % k for k in range(8)]
                    for f in range(32):
                        pb, pn = ps[1 + pi % 6], 'ps%d' % (1 + pi % 6)
                        pi += 1
                        for k in range(8):
                            T.op('pe', lambda k=k: PE.matmul(pb[:, 0:n], lhsT=wup[:, k, f * 128:(f + 1) * 128], rhs=hn[:, k, 0:n],
                                                             start=(k == 0), stop=(k == 7)), r=hn_r + ['wup_%d_%d' % (k, f // 8)], w=[pn], sig=(k == 7))
                        r_, rn = rl[f % 2], 'rlC%d' % (f % 2)
                        T.op('act', lambda: A.activation(out=r_[:, 0:n], in_=pb[:, 0:n], func=AF.Relu), r=[pn], w=[rn])
                        T.op('pool', lambda: G.tensor_tensor(out=u[:, f, 0:n], in0=r_[:, 0:n], in1=r_[:, 0:n], op=ALU.mult),
                             r=[rn], w=['uC_%d' % f])
                    u_r = ['uC_%d' % f for f in range(32)]
                    for dc in range(8):
                        pb, pn = ps[1 + pi % 6], 'ps%d' % (1 + pi % 6)
                        pi += 1
                        for f in range(32):
                            T.op('pe', lambda f=f: PE.matmul(pb[:, 0:n], lhsT=wdown[:, f, dc * 128:(dc + 1) * 128], rhs=u[:, f, 0:n],
                                                             start=(f == 0), stop=(f == 31)), r=u_r + ['wdown_%d' % f], w=[pn], sig=(f == 31))
                        T.op('dve', lambda: V.tensor_tensor(out=hc[b][:, dc, 0:n], in0=pb[:, 0:n], in1=hc[b][:, dc, 0:n], op=ALU.add),
                             r=[pn, hcn + '_%d' % dc], w=[hcn + '_%d' % dc])
                    if not last or dbg:
                        T.dma('sp', hT_v[:, :, t0:t0 + n], hc[b][:, :, 0:n], r=hk_all + [hcn])
                    if last and t0 < S:
                        for k in range(8):
                            T.op('act' if k % 2 == 0 else 'pool',
                                 (lambda k=k: A.activation(out=u[:, k, 0:n], in_=hc[b][:, k, 0:n], func=AF.Square)) if k % 2 == 0 else
                                 (lambda k=k: G.tensor_tensor(out=u[:, k, 0:n], in0=hc[b][:, k, 0:n], in1=hc[b][:, k, 0:n], op=ALU.mult)),
                                 r=[hcn + '_%d' % k], w=['uC_%d' % k])
                        for k in range(8):
                            T.op('pe', lambda k=k: PE.matmul(ps[0][:, 0:n], lhsT=onesD[:, :], rhs=u[:, k, 0:n], start=(k == 0), stop=(k == 7)),
                                 r=['uC_%d' % k, 'onesD'], w=['ps0'], sig=(k == 7))
                        T.op('act', lambda: A.activation(out=rstd[:, 0:n], in_=ps[0][:, 0:n], func=AF.Sqrt, bias=cst[:, 0:1], scale=1.0),
                             r=['ps0', 'cst'], w=['rstd'])
                        T.op('dve', lambda: V.reciprocal(out=rstd[:, 0:n], in_=rstd[:, 0:n]), r=['rstd'], w=['rstd'])
                        for k in range(8):
                            T.op('dve', lambda k=k: V.scalar_tensor_tensor(out=yo[:, k, 0:n], in0=hc[b][:, k, 0:n], scalar=pvec[:, 64 + k:65 + k],
                                                                           in1=rstd[:, 0:n], op0=ALU.mult, op1=ALU.mult),
                                 r=[hcn + '_%d' % k, 'rstd', 'pvec'], w=['yo_%d' % k])
                        for tt in range(n // 128):
                            o_, on_ = ot[0], 'ot0'
                            for k in range(8):
                                pb, pn = ps[1 + pi % 6], 'ps%d' % (1 + pi % 6)
                                pi += 1
                                T.op('pe', lambda: PE.transpose(pb[:, 0:128], yo[:, k, tt * 128:(tt + 1) * 128], ident[:, :]),
                                     r=['yo_%d' % k, 'ident'], w=[pn])
                                if k % 2 == 0:
                                    T.op('dve', lambda: V.tensor_copy(out=o_[:, k * 128:(k + 1) * 128], in_=pb[:, 0:128]), r=[pn, on_], w=[on_ + '_%d' % k])
                                else:
                                    T.op('act', lambda: A.copy(out=o_[:, k * 128:(k + 1) * 128], in_=pb[:, 0:128]), r=[pn, on_], w=[on_ + '_%d' % k])
                            T.dma('sp', out_d.ap()[t0 + tt * 128:t0 + (tt + 1) * 128, :], o_[:, :],
                                  r=[on_ + '_%d' % k for k in range(8)], w=[on_])
                T.barrier()

        @block.sync
        def _(sync):
            body()
    print('program instructions (incl waits):', T.ninstr, 'ops', T.nops, 'waits', T.nwaits, 'cnt', T.cnt, flush=True)
    return nc


def host_constants():
    cmat = np.zeros((4, 128, 128), np.float32)
    cmat[0] = np.eye(128, dtype=np.float32)
    cmat[1] = np.eye(128, dtype=np.float32)[::-1]
    perm = np.zeros((128, 128), np.float32)
    for dp in range(128):
        d = dp + 16 if (dp % 32) < 16 else dp - 16
        perm[d, dp] = 1.0
    cmat[2] = perm
    rel = np.arange(-RB, RB)
    bk = rel_bucket_np(rel)
    oh = np.zeros((32, 2048), np.float32)
    oh[bk, np.arange(2048)] = 1.0
    msk = np.broadcast_to((np.abs(rel) <= 128).astype(np.float32)[None], (20, 2048)).copy()
    tok = np.arange(L)
    real = tok < S
    rows = np.where(real, tok // 64, 0).astype(np.float32)
    cols = np.where(real, tok % 64, 0).astype(np.float32)
    inv = (np.float32(10000.0) ** (-np.arange(0, 32, 2, dtype=np.float32) / np.float32(32))).astype(np.float32)
    ang_r = rows[:, None] * inv[None]
    ang_c = cols[:, None] * inv[None]
    cosT = np.zeros((64, L), np.float32)
    sinT = np.zeros((64, L), np.float32)
    for d in range(64):
        ang = ang_r if d < 32 else ang_c
        i = d % 16
        first = (d % 32) < 16
        cosT[d] = np.cos(ang[:, i])
        sinT[d] = (-np.sin(ang[:, i])) if first else np.sin(ang[:, i])
    rope = np.stack([np.concatenate([cosT, cosT], 0), np.concatenate([sinT, sinT], 0)], 0).astype(np.float32)
    return cmat, oh, msk, rope


def host_layout(inp):
    cmat, oh, msk, rope = host_constants()
    we = inp["w_in_even"]
    kb = we[:, :, 2048:2176]
    wine = np.concatenate([we[:, :, 0:512], we[:, :, 512:1024], we[:, :, 1536:2048],
                           kb[:, :, 0:64], kb[:, :, 0:64], kb[:, :, 64:128], kb[:, :, 64:128],
                           we[:, :, 1024:1536], we[:, :, 2176:2304]], axis=2)
    wo = inp["w_in_odd"]
    kc = wo[:, :, 1024:1152]
    wino = np.concatenate([wo[:, :, 0:1024], kc[:, :, 0:64], kc[:, :, 0:64], kc[:, :, 64:128], kc[:, :, 64:128],
                           wo[:, :, 1152:1280]], axis=2)
    pvec = np.zeros((128, 96), np.float32)
    for i in range(4):
        pvec[:, i * 8:(i + 1) * 8] = inp["norm_attn"][i].reshape(8, 128).T
        pvec[:, 32 + i * 8:32 + (i + 1) * 8] = inp["norm_mlp"][i].reshape(8, 128).T
    pvec[:, 64:72] = inp["norm_final"].reshape(8, 128).T
    for e in range(2):
        pvec[:, 72 + e] = inp["diff_subln"][e]
        pvec[:, 74 + 2 * e] = np.concatenate([inp["qk_norm"][e, 0], inp["qk_norm"][e, 0]])
        pvec[:, 75 + 2 * e] = np.concatenate([inp["qk_norm"][e, 1], inp["qk_norm"][e, 1]])
    shared = {
        "meta": np.ascontiguousarray(inp["meta_tokens"], np.float32),
        "rel_table": np.ascontiguousarray(inp["rel_table"], np.float32),
        "pvec": pvec,
        "diff_lambda": np.ascontiguousarray(inp["diff_lambda"], np.float32).reshape(1, 512),
        "sinks": np.ascontiguousarray(inp["sinks"], np.float32).reshape(1, 32),
        "cmat": cmat, "onehot": oh, "bandmask": msk, "rope": rope,
    }
    wts = {}
    for ly in range(4):
        wts["w_in%d" % ly] = np.ascontiguousarray((wine if ly % 2 == 0 else wino)[ly // 2], np.float32)
        wts["w_out%d" % ly] = np.ascontiguousarray(inp["w_out_even" if ly % 2 == 0 else "w_out_odd"][ly // 2], np.float32)
        wts["w_up%d" % ly] = np.ascontiguousarray(inp["w_up"][ly], np.float32)
        wts["w_down%d" % ly] = np.ascontiguousarray(inp["w_down"][ly], np.float32)
    shared["_wts"] = wts
    return shared


_NC_CACHE = {}
LAUNCH_GROUPS = [[0, 1, 2, 3]]


def kernel(**inputs):
    inp = {k: np.asarray(v) for k, v in inputs.items()}
    shared = host_layout(inp)
    x = np.ascontiguousarray(inp["x"], np.float32)
    nb = x.shape[0]
    meta = shared.pop("meta")
    wts = shared.pop("_wts")
    hT = None
    out = None
    for gi, grp in enumerate(LAUNCH_GROUPS):
        key = tuple(grp)
        if key not in _NC_CACHE:
            _NC_CACHE[key] = build_program(layers=list(grp), total_layers=4)
        nc = _NC_CACHE[key]
        in_maps = []
        for b in range(nb):
            m = dict(shared)
            for ly in grp:
                for nm in ("w_in", "w_out", "w_up", "w_down"):
                    m["%s%d" % (nm, ly)] = wts["%s%d" % (nm, ly)]
            if grp[0] == 0:
                m["x"] = x[b]
                m["meta"] = meta
            else:
                m["hT_in"] = hT[b]
            in_maps.append(m)
        res = run_bass_kernel_spmd(nc, in_maps, core_ids=list(range(nb)))
        if grp[-1] == 3:
            out = np.stack([np.asarray(r["out"], np.float32) for r in res.results], axis=0)
        else:
            hT = [np.ascontiguousarray(np.asarray(r["hT"], np.float32)) for r in res.results]
    return out
```

```python
import math
import os
import numpy as np
import ml_dtypes
import concourse.bass as bass
import concourse.mybir as mybir
from concourse.bass_utils import run_bass_kernel_spmd

F32 = mybir.dt.float32
BF16 = mybir.dt.bfloat16
AF = mybir.ActivationFunctionType
ALU = mybir.AluOpType

L = 4112
S = 4096
NM = 16
DM = 1024
DFF = 4096
EPS = 1e-6
CH512 = [(i * 512, 512) for i in range(8)] + [(4096, 16)]
CH256 = [(i * 256, 256) for i in range(16)] + [(4096, 16)]
KTL = [(i * 128, 128) for i in range(32)] + [(4096, 16)]
NW_EVEN = 14 * 128 + 640
NW_ODD = 10 * 128 + 128
LA = 2
RB = 1024


def tok_pos(t):
    return t + 16 if t < S else t - S


def rel_bucket_np(rel):
    nb = 16
    max_exact = 8
    n = np.abs(rel)
    nf = np.maximum(n, 1).astype(np.float32)
    large = max_exact + (np.log(nf / np.float32(max_exact)) / np.float32(math.log(128 / 8))
                         * np.float32(nb - max_exact)).astype(np.int32)
    large = np.minimum(large, nb - 1)
    return np.where(rel > 0, nb, 0) + np.where(n < max_exact, n, large)


class Trk:
    K = 8

    def __init__(self, nc, sems):
        self.nc = nc
        self.E = {'pe': nc.tensor, 'act': nc.scalar, 'dve': nc.vector, 'pool': nc.gpsimd, 'sp': nc.sync}
        self.sem = {e: sems['c_' + e] for e in self.E}
        self.cnt = {e: 0 for e in self.E}
        self.pend = {e: False for e in self.E}
        self.dsem = {q: [sems['d_%s_%d' % (q, i)] for i in range(self.K)] for q in ('sp', 'pool')}
        self.dval = {q: [0] * self.K for q in ('sp', 'pool')}
        self.dnext = {q: 0 for q in ('sp', 'pool')}
        self.seen = {e: {} for e in self.E}
        self.res = {}
        self.ninstr = 0
        self.nops = {e: 0 for e in self.E}
        self.nwaits = {e: 0 for e in self.E}

    def _wait(self, e, tok):
        if tok[0] == 'c':
            key = tok[1]
            if tok[1] == e and e in ('pe', 'sp'):
                return
            if tok[2] > self.cnt[tok[1]]:
                raise RuntimeError('wait on unresolved token %r (cnt %d)' % (tok, self.cnt[tok[1]]))
            sem = self.sem[tok[1]]
        else:
            key = (tok[1], tok[2])
            sem = self.dsem[tok[1]][tok[2]]
        val = tok[-1]
        if self.seen[e].get(key, 0) >= val:
            return
        self.E[e].wait_ge(sem, val)
        self.seen[e][key] = val
        self.ninstr += 1
        self.nwaits[e] += 1

    def _deps(self, r, w):
        deps = []
        for name in r:
            x = self.res.get(name)
            if x is not None and x[0] is not None:
                deps.append(x[0])
        for name in w:
            x = self.res.get(name)
            if x is not None:
                if x[0] is not None:
                    deps.append(x[0])
                deps.extend(x[1].values())
        return deps

    def _record(self, tok, r, w):
        key = tok[1] if tok[0] == 'c' else (tok[1], tok[2])
        for name in w:
            self.res[name] = [tok, {}]
        for name in r:
            x = self.res.get(name)
            if x is None:
                x = [None, {}]
                self.res[name] = x
            x[1][key] = tok

    def op(self, e, fn, r=(), w=(), sig=True):
        for tok in self._deps(r, w):
            self._wait(e, tok)
        ins = fn()
        self.ninstr += 1
        self.nops[e] += 1
        if sig:
            ins.then_inc(self.sem[e], 1)
            self.cnt[e] += 1
            self.pend[e] = False
            tok = ('c', e, self.cnt[e])
        else:
            self.pend[e] = True
            tok = ('c', e, self.cnt[e] + 1)
        self._record(tok, r, w)
        return tok

    def dma(self, q, out, in_, r=(), w=()):
        for tok in self._deps(r, w):
            self._wait(q, tok)
        slot = self.dnext[q] % self.K
        self.dnext[q] += 1
        if self.dval[q][slot] > 0:
            self._wait(q, ('d', q, slot, self.dval[q][slot]))
        self.E[q].dma_start(out=out, in_=in_).then_inc(self.dsem[q][slot], 16)
        self.ninstr += 1
        self.nops[q] += 1
        self.dval[q][slot] += 16
        tok = ('d', q, slot, self.dval[q][slot])
        self._record(tok, r, w)
        return tok

    def barrier(self):
        for e in self.E:
            assert not self.pend[e], e
        toks = [('c', e, self.cnt[e]) for e in self.E if self.cnt[e] > 0]
        for q in ('sp', 'pool'):
            for i in range(self.K):
                if self.dval[q][i] > 0:
                    toks.append(('d', q, i, self.dval[q][i]))
        for e in self.E:
            for tok in toks:
                self._wait(e, tok)
        self.res = {}


def build_program(n_layers=4, dbg=False, stop=None, layers=None, total_layers=4):
    nc = bass.Bass("TRN2", target_bir_lowering=False)
    dr = {}

    def din(name, shape, dt=F32):
        dr[name] = nc.dram_tensor(name, list(shape), dt, kind="ExternalInput")
        return dr[name]

    if layers is None:
        layers = list(range(int(os.environ.get('START_LAYER', '0')), n_layers))
        total_layers = n_layers
    first = (layers[0] == 0) or bool(os.environ.get('START_LAYER'))
    has_last = (layers[-1] == total_layers - 1)
    if first:
        x_d = din("x", [S, DM])
        meta_d = din("meta", [NM, DM])
    else:
        hin_d = din("hT_in", [DM, L])
    tab_d = din("rel_table", [32, 20])
    pvec_d = din("pvec", [128, 96])
    lam_d = din("diff_lambda", [1, 512])
    sinks_d = din("sinks", [1, 32])
    for ly in layers:
        din("w_in%d" % ly, [DM, NW_EVEN if ly % 2 == 0 else NW_ODD])
        din("w_out%d" % ly, [DM, DM])
        din("w_up%d" % ly, [DM, DFF])
        din("w_down%d" % ly, [DFF, DM])
    cmat_d = din("cmat", [4, 128, 128])
    oh_d = din("onehot", [32, 2048])
    msk_d = din("bandmask", [20, 2048])
    rope_d = din("rope", [2, 128, L])
    if has_last:
        out_d = nc.dram_tensor("out", [S, DM], F32, kind="ExternalOutput")
    hT_d = nc.dram_tensor("hT", [DM, L], F32, **({"kind": "ExternalOutput"} if (dbg or not has_last) else {}))
    qkT_d = nc.dram_tensor("qkT", [14 * 128, L], BF16, **({"kind": "ExternalOutput"} if dbg else {}))
    Vd_d = nc.dram_tensor("Vd", [8, 128, 33, 128], BF16, **({"kind": "ExternalOutput"} if dbg else {}))
    mixT_d = nc.dram_tensor("mixT", [DM, L], BF16, **({"kind": "ExternalOutput"} if dbg else {}))
    EF_d = nc.dram_tensor("EFd", [2, 20, 2048], F32)

    sem_names = ['c_pe', 'c_act', 'c_dve', 'c_pool', 'c_sp'] + \
        ['d_%s_%d' % (q, i) for q in ('sp', 'pool') for i in range(Trk.K)]

    from contextlib import ExitStack
    with ExitStack() as top:
        sems = {n: top.enter_context(nc.semaphore(n)) for n in sem_names}
        ps = [top.enter_context(nc.psum_tensor("ps%d" % i, [128, 512], F32)) for i in range(8)]

        _uid = [0]

        def sb(stack, name, shape, dt):
            _uid[0] += 1
            return stack.enter_context(nc.sbuf_tensor("%s_%d" % (name, _uid[0]), list(shape), dt))

        ident = sb(top, "ident", [128, 128], F32)
        Jm = sb(top, "Jm", [128, 128], F32)
        permf = sb(top, "permf", [128, 128], F32)
        permb = sb(top, "permb", [128, 128], BF16)
        onesD = sb(top, "onesD", [128, 128], BF16)
        ones128 = sb(top, "ones128", [128, 128], BF16)
        ones1 = sb(top, "ones1", [128, 128], BF16)
        blk64 = sb(top, "blk64", [128, 128], BF16)
        pvec = sb(top, "pvec_sb", [128, 96], F32)
        pv2 = sb(top, "pv2", [128, 16], F32)
        tabbc = sb(top, "tabbc", [128, 640], F32)
        lamv = sb(top, "lamv", [128, 512], F32)
        esink = sb(top, "esink", [128, 32], F32)
        cst = sb(top, "cst", [128, 4], F32)
        block = top.enter_context(nc.Block())

        T = Trk(nc, sems)
        E = T.E
        V, A, G, PE = nc.vector, nc.scalar, nc.gpsimd, nc.tensor

        def body():
            T.dma('sp', ident[:], dr["cmat"].ap()[0], w=['ident'])
            T.dma('sp', Jm[:], dr["cmat"].ap()[1], w=['Jm'])
            T.dma('sp', permf[:], dr["cmat"].ap()[2], w=['permf'])
            T.dma('sp', pvec[:], pvec_d.ap()[:, :], w=['pvec'])
            T.dma('sp', tabbc[:], bass.AP(tab_d, 0, [[0, 128], [1, 640]]), w=['tabbc'])
            T.dma('sp', lamv[:], bass.AP(lam_d, 0, [[0, 128], [1, 512]]), w=['lamv'])
            T.dma('sp', esink[:], bass.AP(sinks_d, 0, [[0, 128], [1, 32]]), w=['esink'])
            T.op('dve', lambda: V.tensor_copy(out=permb[:], in_=permf[:]), r=['permf'], w=['permb'])
            T.op('pool', lambda: G.memset(onesD[:], 1.0 / 1024), w=['onesD'])
            T.op('pool', lambda: G.memset(ones128[:], 1.0 / 128), w=['ones128'])
            T.op('pool', lambda: G.memset(ones1[:], 1.0), w=['ones1'])
            T.op('pool', lambda: G.memset(blk64[:], 0.0), w=['blk64'])
            T.op('pool', lambda: G.memset(blk64[0:64, 0:64], 1.0 / 64), w=['blk64'])
            T.op('pool', lambda: G.memset(blk64[64:128, 64:128], 1.0 / 64), w=['blk64'])
            T.op('pool', lambda: G.memset(cst[:, 0:1], EPS), w=['cst'])
            T.op('act', lambda: A.activation(out=esink[:], in_=esink[:], func=AF.Exp), r=['esink'], w=['esink'])
            with ExitStack() as st:
                tab_sb = sb(st, "tab_sb", [32, 20], F32)
                oh = sb(st, "oh", [32, 2048], F32)
                ef = sb(st, "ef", [20, 2048], F32)
                efm = sb(st, "efm", [20, 2048], F32)
                msk = sb(st, "msk", [20, 2048], F32)
                T.dma('sp', tab_sb[:], tab_d.ap()[:, :], w=['tab_sb'])
                T.dma('sp', oh[:], oh_d.ap()[:, :], w=['oh'])
                T.dma('sp', msk[:], msk_d.ap()[:, :], w=['msk'])
                for c in range(4):
                    T.op('pe', lambda c=c: PE.matmul(ps[c][0:20, :], lhsT=tab_sb[:, :], rhs=oh[:, c * 512:(c + 1) * 512],
                                                     start=True, stop=True), r=['tab_sb', 'oh'], w=['ps%d' % c])
                    T.op('act', lambda c=c: A.activation(out=ef[:, c * 512:(c + 1) * 512], in_=ps[c][0:20, :], func=AF.Exp),
                         r=['ps%d' % c], w=['ef%d' % c])
                T.op('dve', lambda: V.tensor_tensor(out=efm[:], in0=ef[:], in1=msk[:], op=ALU.mult),
                     r=['ef0', 'ef1', 'ef2', 'ef3', 'msk'], w=['efm'])
                T.dma('sp', EF_d.ap()[0], ef[:], r=['ef0', 'ef1', 'ef2', 'ef3'])
                T.dma('sp', EF_d.ap()[1], efm[:], r=['efm'])
                T.barrier()

            if first:
                with ExitStack() as st:
                    xin = [sb(st, "xin%d" % i, [128, DM], F32) for i in range(2)]
                    xT = [sb(st, "xT%d" % i, [128, 8, 128], F32) for i in range(2)]
                    hT_v = hT_d.ap().rearrange("(k p) t -> p k t", p=128)
                    for ti, (t0, rows) in enumerate(KTL):
                        b = ti % 2
                        src = x_d.ap()[t0:t0 + rows, :] if t0 < S else meta_d.ap()[:, :]
                        T.dma('sp', xin[b][0:rows, :], src, w=['xin%d' % b])
                        for k in range(8):
                            pb = ps[(ti * 8 + k) % 4]
                            pn = 'ps%d' % ((ti * 8 + k) % 4)
                            T.op('pe', lambda: PE.transpose(pb[:, 0:rows], xin[b][0:rows, k * 128:(k + 1) * 128],
                                                            ident[0:rows, 0:rows]), r=['xin%d' % b, 'ident'], w=[pn])
                            if k % 2 == 0:
                                T.op('dve', lambda: V.tensor_copy(out=xT[b][:, k, 0:rows], in_=pb[:, 0:rows]),
                                     r=[pn], w=['xT%d_%d' % (b, k)])
                            else:
                                T.op('act', lambda: A.copy(out=xT[b][:, k, 0:rows], in_=pb[:, 0:rows]),
                                     r=[pn], w=['xT%d_%d' % (b, k)])
                        T.dma('sp', hT_v[:, :, t0:t0 + rows], xT[b][:, :, 0:rows], r=['xT%d_%d' % (b, k) for k in range(8)])
                    T.barrier()
            else:
                with ExitStack() as st:
                    cp = [sb(st, "cp%d" % i, [128, 8, 512], F32) for i in range(2)]
                    hT_v = hT_d.ap().rearrange("(k p) t -> p k t", p=128)
                    hin_v = hin_d.ap().rearrange("(k p) t -> p k t", p=128)
                    for ci, (t0, n) in enumerate(CH512):
                        b = ci % 2
                        T.dma('sp', cp[b][:, :, 0:n], hin_v[:, :, t0:t0 + n], w=['cp%d' % b])
                        T.dma('sp', hT_v[:, :, t0:t0 + n], cp[b][:, :, 0:n], r=['cp%d' % b])
                    T.barrier()

            for layer in layers:
                even = (layer % 2 == 0)
                li = layer // 2
                lastl = (layer == layers[-1])
                if stop == 'p0' and lastl:
                    break
                phase_a(layer, even, li)
                if stop == 'A' and lastl:
                    break
                if even:
                    phase_b_even(layer, li)
                else:
                    phase_b_odd(layer, li)
                if stop == 'B' and lastl:
                    break
                phase_c(layer, even, li, last=(layer == total_layers - 1))
            T.barrier()

        def cast(i, dst, src, srcn, dstn):
            e = ('pool', 'dve', 'act')[i % 3]
            if e == 'pool':
                T.op('pool', lambda: G.tensor_copy(out=dst, in_=src), r=[srcn], w=[dstn])
            elif e == 'dve':
                T.op('dve', lambda: V.tensor_copy(out=dst, in_=src), r=[srcn], w=[dstn])
            else:
                T.op('act', lambda: A.copy(out=dst, in_=src), r=[srcn], w=[dstn])

        def emit_norm(hc, hcn, n, gcol, hn, hnn, sq, sqn, rstd, psb, psn):
            for k in range(8):
                T.op('act' if k % 2 == 0 else 'pool',
                     (lambda k=k: A.activation(out=sq[:, k, 0:n], in_=hc[:, k, 0:n], func=AF.Square)) if k % 2 == 0 else
                     (lambda k=k: G.tensor_tensor(out=sq[:, k, 0:n], in0=hc[:, k, 0:n], in1=hc[:, k, 0:n], op=ALU.mult)),
                     r=[hcn], w=['%s_%d' % (sqn, k)])
            for k in range(8):
                T.op('pe', lambda k=k: PE.matmul(psb[:, 0:n], lhsT=onesD[:, :], rhs=sq[:, k, 0:n], start=(k == 0), stop=(k == 7)),
                     r=['%s_%d' % (sqn, k), 'onesD'], w=[psn], sig=(k == 7))
            T.op('act', lambda: A.activation(out=rstd[:, 0:n], in_=psb[:, 0:n], func=AF.Sqrt, bias=cst[:, 0:1], scale=1.0),
                 r=[psn, 'cst'], w=['rstd'])
            T.op('dve', lambda: V.reciprocal(out=rstd[:, 0:n], in_=rstd[:, 0:n]), r=['rstd'], w=['rstd'])
            for k in range(8):
                T.op('dve', lambda k=k: V.scalar_tensor_tensor(out=hn[:, k, 0:n], in0=hc[:, k, 0:n], scalar=pvec[:, gcol + k:gcol + k + 1],
                                                               in1=rstd[:, 0:n], op0=ALU.mult, op1=ALU.mult),
                     r=[hcn, 'rstd', 'pvec'], w=['%s_%d' % (hnn, k)])

        def phase_a(layer, even, li):
            NW = NW_EVEN if even else NW_ODD
            NFM = 14 if even else 10
            VOFF = NFM * 128
            with ExitStack() as st:
                win = sb(st, "win", [128, 8, NW], BF16)
                hc = [sb(st, "hcA%d" % i, [128, 8, 512], F32) for i in range(2)]
                sq = sb(st, "sqA", [128, 8, 512], BF16)
                hn = sb(st, "hnA", [128, 8, 512], BF16)
                rstd = sb(st, "rstdA", [128, 512], F32)
                stg = [sb(st, "stgA%d" % i, [128, 512], BF16) for i in range(4)]
                stgv = [sb(st, "stgvA%d" % i, [128, 1024], BF16) for i in range(2)]
                if even:
                    cosb = sb(st, "cosb", [128, 512], F32)
                    sinb = sb(st, "sinb", [128, 512], F32)
                    sqb = sb(st, "sqb", [128, 512], BF16)
                    rsb = sb(st, "rsb", [128, 512], F32)
                    xn = sb(st, "xn", [128, 512], F32)
                    xnb = sb(st, "xnb", [128, 512], BF16)
                    t1 = sb(st, "t1", [128, 512], F32)
                    t2 = sb(st, "t2", [128, 512], F32)
                wsrc = dr["w_in%d" % layer].ap()
                wstg = [sb(st, "wstgA%d" % i, [128, NW], F32) for i in range(2)]
                for k in range(8):
                    T.dma('sp', wstg[k % 2][:, :], wsrc[k * 128:(k + 1) * 128, :], w=['wstgA%d' % (k % 2)])
                    cast(k, win[:, k, :], wstg[k % 2][:, :], 'wstgA%d' % (k % 2), 'win_%d' % k)
                voff = 512 if even else 0
                for b in range(2):
                    T.op('pool', lambda b=b: G.memset(stgv[b][:, voff:voff + 512], 1.0), w=['stgv%d' % b])
                if even:
                    T.op('dve', lambda: V.tensor_scalar(out=pv2[:, 0:1], in0=pvec[:, 74 + 2 * li:75 + 2 * li], scalar1=0.125,
                                                        scalar2=None, op0=ALU.mult), r=['pvec'], w=['pv2'])
                hT_v = hT_d.ap().rearrange("(k p) t -> p k t", p=128)
                stg_i = 0
                for ci, (t0, n) in enumerate(CH512):
                    b = ci % 2
                    hcn = 'hcA%d' % b
                    T.dma('sp', hc[b][:, :, 0:n], hT_v[:, :, t0:t0 + n], w=[hcn])
                    if even:
                        T.dma('sp', cosb[:, 0:n], rope_d.ap()[0][:, t0:t0 + n], w=['cosb'])
                        T.dma('sp', sinb[:, 0:n], rope_d.ap()[1][:, t0:t0 + n], w=['sinb'])
                    emit_norm(hc[b], hcn, n, layer * 8, hn, 'hnA', sq, 'sqA', rstd, ps[0], 'ps0')
                    hn_r = ['hnA_%d' % k for k in range(8)]
                    for fo in range(NFM if not os.environ.get('SKIP_FM') else 0):
                        pi = 1 + (fo % 3)
                        pb, pn = ps[pi], 'ps%d' % pi
                        for k in range(8):
                            T.op('pe', lambda k=k: PE.matmul(pb[:, 0:n], lhsT=win[:, k, fo * 128:(fo + 1) * 128], rhs=hn[:, k, 0:n],
                                                             start=(k == 0), stop=(k == 7)),
                                 r=hn_r + ['win_%d' % k], w=[pn], sig=(k == 7))
                        sg = stg[stg_i % 4]
                        sgn = 'stgA%d' % (stg_i % 4)
                        stg_i += 1
                        if even and 8 <= fo < 14 and not os.environ.get('SKIP_B'):
                            isq = fo < 12
                            T.op('act', lambda: A.activation(out=sqb[:, 0:n], in_=pb[:, 0:n], func=AF.Square), r=[pn], w=['sqb'])
                            T.op('pe', lambda: PE.matmul(ps[4][:, 0:n], lhsT=blk64[:, :], rhs=sqb[:, 0:n], start=True, stop=True),
                                 r=['sqb', 'blk64'], w=['ps4'])
                            T.op('act', lambda: A.activation(out=rsb[:, 0:n], in_=ps[4][:, 0:n], func=AF.Sqrt, bias=cst[:, 0:1], scale=1.0),
                                 r=['ps4', 'cst'], w=['rsb'])
                            T.op('dve', lambda: V.reciprocal(out=rsb[:, 0:n], in_=rsb[:, 0:n]), r=['rsb'], w=['rsb'])
                            gap = pv2[:, 0:1] if isq else pvec[:, 75 + 2 * li:76 + 2 * li]
                            T.op('dve', lambda: V.scalar_tensor_tensor(out=xn[:, 0:n], in0=pb[:, 0:n], scalar=gap, in1=rsb[:, 0:n],
                                                                       op0=ALU.mult, op1=ALU.mult),
                                 r=[pn, 'rsb', 'pv2', 'pvec'], w=['xn'])
                            T.op('act', lambda: A.copy(out=xnb[:, 0:n], in_=xn[:, 0:n]), r=['xn'], w=['xnb'])
                            T.op('pe', lambda: PE.matmul(ps[5][:, 0:n], lhsT=permb[:, :], rhs=xnb[:, 0:n], start=True, stop=True),
                                 r=['xnb', 'permb'], w=['ps5'])
                            T.op('pool', lambda: G.tensor_tensor(out=t1[:, 0:n], in0=xn[:, 0:n], in1=cosb[:, 0:n], op=ALU.mult),
                                 r=['xn', 'cosb'], w=['t1'])
                            T.op('dve', lambda: V.tensor_tensor(out=t2[:, 0:n], in0=ps[5][:, 0:n], in1=sinb[:, 0:n], op=ALU.mult),
                                 r=['ps5', 'sinb'], w=['t2'])
                            T.op('pool', lambda: G.tensor_tensor(out=sg[:, 0:n], in0=t1[:, 0:n], in1=t2[:, 0:n], op=ALU.add),
                                 r=['t1', 't2'], w=[sgn])
                        else:
                            isq = (fo < 4) if even else (fo < 8)
                            if isq:
                                T.op('act', lambda: A.activation(out=sg[:, 0:n], in_=pb[:, 0:n], func=AF.Copy, scale=0.125), r=[pn], w=[sgn])
                            else:
                                T.op('dve', lambda: V.tensor_copy(out=sg[:, 0:n], in_=pb[:, 0:n]), r=[pn], w=[sgn])
                        T.dma('sp', qkT_d.ap()[fo * 128:(fo + 1) * 128, t0:t0 + n], sg[:, 0:n], r=[sgn])
                    ntile = max(1, n // 128)
                    for tt in range(ntile if not os.environ.get('SKIP_V') else 0):
                        rows = min(128, n)
                        tile_idx = (t0 // 128) + tt
                        sv = stgv[tile_idx % 2]
                        svn = 'stgv%d' % (tile_idx % 2)
                        tsl = slice(tt * 128, tt * 128 + rows)
                        if even:
                            for k in range(8):
                                T.op('pe', lambda k=k: PE.matmul(ps[6][0:rows, :], lhsT=hn[:, k, tsl], rhs=win[:, k, VOFF:VOFF + 512],
                                                                 start=(k == 0), stop=(k == 7)), r=hn_r + ['win_%d' % k], w=['ps6'], sig=(k == 7))
                            T.op('act', lambda: A.copy(out=sv[0:rows, 0:512], in_=ps[6][0:rows, :]), r=['ps6'], w=[svn + 'a'])
                            vb0 = VOFF + 512
                        else:
                            vb0 = VOFF
                        for k in range(8):
                            T.op('pe', lambda k=k: PE.matmul(ps[7][0:rows, 0:128], lhsT=hn[:, k, tsl], rhs=win[:, k, vb0:vb0 + 128],
                                                             start=(k == 0), stop=(k == 7)), r=hn_r + ['win_%d' % k], w=['ps7'], sig=(k == 7))
                        for j in range(2):
                            T.op('dve', lambda j=j: V.tensor_copy(out=sv[0:rows, voff + j * 128:voff + j * 128 + 64],
                                                                  in_=ps[7][0:rows, j * 64:(j + 1) * 64]), r=['ps7', svn], w=[svn + 'b%d' % j])
                            T.op('dve', lambda j=j: V.tensor_copy(out=sv[0:rows, voff + 256 + j * 128 + 64:voff + 256 + (j + 1) * 128],
                                                                  in_=ps[7][0:rows, j * 64:(j + 1) * 64]), r=['ps7', svn], w=[svn + 'c%d' % j])
                        ns = 8 if even else 4
                        for s_ in range(ns):
                            T.dma('sp', Vd_d.ap()[s_, 0:rows, tile_idx, :], sv[0:rows, s_ * 128:(s_ + 1) * 128],
                                  r=[svn + 'a', svn + 'b0', svn + 'b1', svn + 'c0', svn + 'c1', svn])
                T.barrier()

        def flip(dst_ap, hk, m0, nk, nq, psb, psn, dstn, use_act=False):
            T.op('pe', lambda: PE.matmul(psb[0:nk, 0:nq], lhsT=hk[:, m0:m0 + nk], rhs=Jm[:, 0:nq], start=True, stop=True),
                 r=['hk', 'Jm'], w=[psn])
            T.op('dve', lambda: V.tensor_copy(out=dst_ap, in_=psb[0:nk, 0:nq]), r=[psn], w=[dstn])

        def bucket_const(kmin, kmax, qmin, qmax):
            rel = np.arange(kmin - qmax, kmax - qmin + 1)
            b = rel_bucket_np(rel)
            if np.all(b == b[0]):
                return int(b[0])
            return None

        def phase_b_even(layer, li):
            lam_init = 0.8 - 0.6 * math.exp(-0.3 * layer)
            with ExitStack() as st:
                QT = sb(st, "QT", [128, L], BF16)
                KT = sb(st, "KT", [128, L], BF16)
                Vt = sb(st, "Vt", [128, 33, 128], BF16)
                Vt2 = sb(st, "Vt2", [128, 33, 128], BF16)
                hk = sb(st, "hk", [128, 1152], F32)
                ETr = sb(st, "ETr", [128, 6, 512], F32)
                ETmk = sb(st, "ETmk", [16, 512], F32)
                ETq0 = sb(st, "ETq0", [128, 16], F32)
                ETmm = sb(st, "ETmm", [16, 16], F32)
                PT = [sb(st, "PT%d" % i, [128, 512], BF16) for i in range(3)]
                PTf = [sb(st, "PTf%d" % i, [128, 512], F32) for i in range(2)]
                rc = [sb(st, "rc%d" % i, [128, 512], F32) for i in range(2)]
                oo = [sb(st, "oo%d" % i, [128, 512], F32) for i in range(2)]
                dd = sb(st, "dd", [128, 512], F32)
                sqd = sb(st, "sqd", [128, 512], BF16)
                rsd = sb(st, "rsd", [128, 512], F32)
                stg = [sb(st, "stgB%d" % i, [128, 512], BF16) for i in range(2)]
                tmp = sb(st, "tmpl", [128, 256], F32)
                lo = li * 256
                T.op('dve', lambda: V.tensor_tensor(out=tmp[:, 0:64], in0=lamv[:, lo:lo + 64], in1=lamv[:, lo + 64:lo + 128], op=ALU.mult),
                     r=['lamv'], w=['tmpl'])
                T.op('dve', lambda: V.tensor_tensor(out=tmp[:, 64:128], in0=lamv[:, lo + 128:lo + 192], in1=lamv[:, lo + 192:lo + 256], op=ALU.mult),
                     r=['lamv', 'tmpl'], w=['tmpl'])
                T.op('dve', lambda: V.reduce_sum(out=tmp[:, 128:129], in_=tmp[:, 0:64], axis=mybir.AxisListType.X), r=['tmpl'], w=['tmpl'])
                T.op('dve', lambda: V.reduce_sum(out=tmp[:, 129:130], in_=tmp[:, 64:128], axis=mybir.AxisListType.X), r=['tmpl'], w=['tmpl'])
                T.op('act', lambda: A.activation(out=tmp[:, 130:132], in_=tmp[:, 128:130], func=AF.Exp), r=['tmpl'], w=['tmpl'])
                T.op('dve', lambda: V.tensor_tensor(out=tmp[:, 132:133], in0=tmp[:, 131:132], in1=tmp[:, 130:131], op=ALU.subtract),
                     r=['tmpl'], w=['tmpl'])
                T.op('dve', lambda: V.tensor_scalar(out=pv2[:, 1:2], in0=tmp[:, 132:133], scalar1=-lam_init, scalar2=None, op0=ALU.add),
                     r=['tmpl'], w=['pv2'])
                T.op('dve', lambda: V.tensor_scalar(out=pv2[:, 2:3], in0=pvec[:, 72 + li:73 + li], scalar1=(1.0 - lam_init), scalar2=None,
                                                    op0=ALU.mult), r=['pvec', 'pv2'], w=['pv2'])
                for h in range(4):
                    T.dma('sp', QT[:, :], qkT_d.ap()[h * 128:(h + 1) * 128, :], w=['QT'])
                    T.dma('sp', KT[:, :], qkT_d.ap()[(4 + h) * 128:(5 + h) * 128, :], w=['KT'])
                    T.dma('sp', Vt[:, :, :], Vd_d.ap()[h], w=['Vt'])
                    T.dma('sp', hk[:, :], bass.AP(EF_d, h * 2048 + RB - 512 - 127, [[1, 128], [1, 1152]]), w=['hk'])
                    fi = 0
                    for dc in range(6):
                        for s_ in range(4):
                            Dv = 128 * (dc - 1 - s_)
                            flip(ETr[:, dc, s_ * 128:(s_ + 1) * 128], hk, Dv + 512, 128, 128, ps[7], 'ps7', 'ETr')
                    for s_ in range(4):
                        flip(ETmk[:, s_ * 128:(s_ + 1) * 128], hk, -16 - 128 * s_ + 512, 16, 128, ps[7], 'ps7', 'ETmk')
                    flip(ETq0[:, :], hk, 16 + 512, 128, 16, ps[7], 'ps7', 'ETq0')
                    flip(ETmm[:, :], hk, 0 + 512, 16, 16, ps[7], 'ps7', 'ETmm')
                    its = [(j, c, t) for j in range(9) for c in range(2) for t in range(33)]

                    def front(i):
                        j, c, t = its[i]
                        q0, n = CH512[j]
                        k0, kr = KTL[t]
                        hs = slice(64 * c, 64 * c + 64)
                        Sb, Sn = ps[i % 3], 'ps%d' % (i % 3)
                        P_, Pn = PT[i % 3], 'PT%d' % (i % 3)
                        T.op('pe', lambda: PE.matmul(Sb[0:kr, 0:n], lhsT=KT[hs, k0:k0 + kr], rhs=QT[hs, q0:q0 + n], start=True, stop=True),
                             r=['KT', 'QT'], w=[Sn])
                        bc = bucket_const(tok_pos(k0), tok_pos(k0 + kr - 1), tok_pos(q0), tok_pos(q0 + n - 1))
                        if bc is not None:
                            col = bc * 20 + h
                            T.op('act', lambda: A.activation(out=P_[0:kr, 0:n], in_=Sb[0:kr, 0:n], func=AF.Exp,
                                                             bias=tabbc[0:kr, col:col + 1], scale=1.0), r=[Sn, 'tabbc'], w=[Pn])
                        else:
                            if t < 32 and j < 8:
                                et, etn = ETr[:, t - 4 * j + 1, :], 'ETr'
                            elif t == 32 and j < 8:
                                et, etn = ETmk[:, :], 'ETmk'
                            elif t < 32:
                                et, etn = ETq0[:, :], 'ETq0'
                            else:
                                et, etn = ETmm[:, :], 'ETmm'
                            pf, pfn = PTf[i % 2], 'PTf%d' % (i % 2)
                            T.op('act', lambda: A.activation(out=pf[0:kr, 0:n], in_=Sb[0:kr, 0:n], func=AF.Exp), r=[Sn], w=[pfn])
                            T.op('dve', lambda: V.tensor_tensor(out=P_[0:kr, 0:n], in0=pf[0:kr, 0:n], in1=et, op=ALU.mult),
                                 r=[pfn, etn], w=[Pn])

                    def back(i):
                        j, c, t = its[i]
                        q0, n = CH512[j]
                        k0, kr = KTL[t]
                        P_, Pn = PT[i % 3], 'PT%d' % (i % 3)
                        Ob, On = ps[3 + c], 'ps%d' % (3 + c)
                        Rb, Rn = ps[5 + c], 'ps%d' % (5 + c)
                        T.op('pe', lambda: PE.matmul(Ob[:, 0:n], lhsT=Vt[0:kr, t, :], rhs=P_[0:kr, 0:n], start=(t == 0), stop=(t == 32)),
                             r=[Pn, 'Vt'], w=[On], sig=(t == 32))
                        T.op('pe', lambda: PE.matmul(Rb[:, 0:n], lhsT=ones1[0:kr, :], rhs=P_[0:kr, 0:n], start=(t == 0), stop=(t == 32)),
                             r=[Pn, 'ones1'], w=[Rn], sig=True)
                        if t < 32:
                            return
                        T.op('dve', lambda: V.reciprocal(out=rc[c][:, 0:n], in_=Rb[:, 0:n]), r=[Rn], w=['rc%d' % c])
                        T.op('dve', lambda: V.tensor_tensor(out=oo[c][:, 0:n], in0=Ob[:, 0:n], in1=rc[c][:, 0:n], op=ALU.mult),
                             r=[On, 'rc%d' % c], w=['oo%d' % c])
                        if c == 0:
                            return
                        T.op('dve', lambda: V.scalar_tensor_tensor(out=dd[:, 0:n], in0=oo[1][:, 0:n], scalar=pv2[:, 1:2], in1=oo[0][:, 0:n],
                                                                   op0=ALU.mult, op1=ALU.add), r=['oo0', 'oo1', 'pv2'], w=['dd'])
                        T.op('act', lambda: A.activation(out=sqd[:, 0:n], in_=dd[:, 0:n], func=AF.Square), r=['dd'], w=['sqd'])
                        T.op('pe', lambda: PE.matmul(ps[7][:, 0:n], lhsT=ones128[:, :], rhs=sqd[:, 0:n], start=True, stop=True),
                             r=['sqd', 'ones128'], w=['ps7'])
                        T.op('act', lambda: A.activation(out=rsd[:, 0:n], in_=ps[7][:, 0:n], func=AF.Sqrt, bias=cst[:, 0:1], scale=1.0),
                             r=['ps7', 'cst'], w=['rsd'])
                        T.op('dve', lambda: V.reciprocal(out=rsd[:, 0:n], in_=rsd[:, 0:n]), r=['rsd'], w=['rsd'])
                        sg, sgn = stg[j % 2], 'stgB%d' % (j % 2)
                        T.op('dve', lambda: V.scalar_tensor_tensor(out=sg[:, 0:n], in0=dd[:, 0:n], scalar=pv2[:, 2:3], in1=rsd[:, 0:n],
                                                                   op0=ALU.mult, op1=ALU.mult), r=['dd', 'rsd', 'pv2'], w=[sgn])
                        T.dma('sp', mixT_d.ap()[h * 128:(h + 1) * 128, q0:q0 + n], sg[:, 0:n], r=[sgn])

                    for i in range(len(its) + LA):
                        if i < len(its):
                            front(i)
                        if i - LA >= 0:
                            back(i - LA)
                ei = 0
                for jg in range(2):
                    T.dma('sp', KT[:, :], qkT_d.ap()[(12 + jg) * 128:(13 + jg) * 128, :], w=['KT'])
                    T.dma('sp', Vt[:, :, :], Vd_d.ap()[4 + jg], w=['Vt'])
                    T.dma('sp', Vt2[:, :, :], Vd_d.ap()[6 + jg], w=['Vt2'])
                    for qc in range(2):
                        ch = jg * 2 + qc
                        T.dma('sp', QT[:, :], qkT_d.ap()[(8 + ch) * 128:(9 + ch) * 128, :], w=['QT'])
                        its = [(j, p, t) for j in range(9) for p in range(2) for t in range(33)]

                        def frontb(i):
                            j, p, t = its[i]
                            q0, n = CH512[j]
                            k0, kr = KTL[t]
                            hs = slice(64 * p, 64 * p + 64)
                            Sb, Sn = ps[i % 3], 'ps%d' % (i % 3)
                            P_, Pn = PT[i % 3], 'PT%d' % (i % 3)
                            T.op('pe', lambda: PE.matmul(Sb[0:kr, 0:n], lhsT=KT[hs, k0:k0 + kr], rhs=QT[hs, q0:q0 + n], start=True, stop=True),
                                 r=['KT', 'QT'], w=[Sn])
                            T.op('act', lambda: A.activation(out=P_[0:kr, 0:n], in_=Sb[0:kr, 0:n], func=AF.Exp), r=[Sn], w=[Pn])

                        def backb(i):
                            j, p, t = its[i]
                            q0, n = CH512[j]
                            k0, kr = KTL[t]
                            P_, Pn = PT[i % 3], 'PT%d' % (i % 3)
                            Ob, On = ps[3 + p], 'ps%d' % (3 + p)
                            vt, vtn = (Vt, 'Vt') if p == 0 else (Vt2, 'Vt2')
                            sg, sgn = stg[j % 2], 'stgB%d' % (j % 2)
                            T.op('pe', lambda: PE.matmul(Ob[:, 0:n], lhsT=vt[0:kr, t, :], rhs=P_[0:kr, 0:n], start=(t == 0), stop=(t == 32)),
                                 r=[Pn, vtn], w=[On], sig=True)
                            if t < 32:
                                return
                            os_ = slice(64 * p, 64 * p + 64)
                            rs_ = slice(64 * (1 - p), 64 * (1 - p) + 64)
                            T.op('dve', lambda: V.tensor_copy(out=rc[p][os_, 0:n], in_=Ob[rs_, 0:n]), r=[On], w=['rc%d' % p])
                            T.op('dve', lambda: V.reciprocal(out=rc[p][os_, 0:n], in_=rc[p][os_, 0:n]), r=['rc%d' % p], w=['rc%d' % p])
                            T.op('dve', lambda: V.tensor_tensor(out=sg[os_, 0:n], in0=Ob[os_, 0:n], in1=rc[p][os_, 0:n], op=ALU.mult),
                                 r=[On, 'rc%d' % p, sgn], w=[sgn + '_%d' % p])
                            if p == 1:
                                T.dma('sp', mixT_d.ap()[512 + ch * 128:512 + (ch + 1) * 128, q0:q0 + n], sg[:, 0:n],
                                      r=[sgn + '_0', sgn + '_1'], w=[sgn])

                        for i in range(len(its) + LA):
                            if i < len(its):
                                frontb(i)
                            if i - LA >= 0:
                                backb(i - LA)
                T.barrier()

        def phase_b_odd(layer, li):
            with ExitStack() as st:
                QT = sb(st, "QT", [128, L], BF16)
                KT = sb(st, "KT", [128, L], BF16)
                Vt = sb(st, "Vt", [128, 33, 128], BF16)
                Vt2 = sb(st, "Vt2", [128, 33, 128], BF16)
                hkm = sb(st, "hkm", [128, 384], F32)
                hku = sb(st, "hku", [128, 144], F32)
                ETo = [sb(st, "ETo%d" % p, [128, 384], F32) for p in range(2)]
                ETk0 = [sb(st, "ETk0%d" % p, [16, 128], F32) for p in range(2)]
                ETq0 = [sb(st, "ETq0%d" % p, [128, 16], F32) for p in range(2)]
                ETmm = [sb(st, "ETmm%d" % p, [16, 16], F32) for p in range(2)]
                PT = [sb(st, "PT%d" % i, [128, 384], BF16) for i in range(3)]
                PTf = [sb(st, "PTf%d" % i, [128, 384], F32) for i in range(2)]
                PTm = [sb(st, "PTm%d" % i, [16, 512], BF16) for i in range(2)]
                PTmf = sb(st, "PTmf", [16, 512], F32)
                rc = [sb(st, "rc%d" % i, [128, 512], F32) for i in range(2)]
                stg = [sb(st, "stgB%d" % i, [128, 512], BF16) for i in range(2)]
                ei = 0
                mi = 0
                for ch in range(8):
                    jg = ch // 4
                    if ch % 4 == 0:
                        T.dma('sp', KT[:, :], qkT_d.ap()[(8 + jg) * 128:(9 + jg) * 128, :], w=['KT'])
                        T.dma('sp', Vt[:, :, :], Vd_d.ap()[jg], w=['Vt'])
                        T.dma('sp', Vt2[:, :, :], Vd_d.ap()[2 + jg], w=['Vt2'])
                    T.dma('sp', QT[:, :], qkT_d.ap()[ch * 128:(ch + 1) * 128, :], w=['QT'])
                    for p in range(2):
                        hh = 4 + ch * 2 + p
                        T.dma('sp', hkm[:, :], bass.AP(EF_d, (20 + hh) * 2048 + RB - 128 - 127, [[1, 128], [1, 384]]), w=['hk'])
                        T.dma('sp', hku[:, :], bass.AP(EF_d, hh * 2048 + RB - 16 - 127, [[1, 128], [1, 144]]), w=['hku'])
                        for s_ in range(3):
                            flip(ETo[p][:, s_ * 128:(s_ + 1) * 128], hkm, 128 * (s_ - 1) + 128, 128, 128, ps[7], 'ps7', 'ETo%d' % p)
                        flip(ETq0[p][:, :], hkm, 16 + 128, 128, 16, ps[7], 'ps7', 'ETq0%d' % p)
                        T.op('pe', lambda: PE.matmul(ps[7][0:16, 0:128], lhsT=hku[:, 0:16], rhs=Jm[:, 0:128], start=True, stop=True),
                             r=['hku', 'Jm'], w=['ps7'])
                        T.op('dve', lambda: V.tensor_copy(out=ETk0[p][:, :], in_=ps[7][0:16, 0:128]), r=['ps7'], w=['ETk0%d' % p])
                        T.op('pe', lambda: PE.matmul(ps[7][0:16, 0:16], lhsT=hku[:, 16:32], rhs=Jm[:, 0:16], start=True, stop=True),
                             r=['hku', 'Jm'], w=['ps7'])
                        T.op('dve', lambda: V.tensor_copy(out=ETmm[p][:, :], in_=ps[7][0:16, 0:16]), r=['ps7'], w=['ETmm%d' % p])
                    its = [(j, p, ul) for j in range(8) for p in range(2) for ul in range(4)] + [(8, 0, 0), (8, 1, 0)]
                    pmmap = {}

                    def fronto(i):
                        j, p, ul = its[i]
                        q0, n = CH512[j]
                        hh = 4 + ch * 2 + p
                        hs = slice(64 * p, 64 * p + 64)
                        Sb, Sn = ps[i % 3], 'ps%d' % (i % 3)
                        P_, Pn = PT[i % 3], 'PT%d' % (i % 3)
                        pf, pfn = PTf[i % 2], 'PTf%d' % (i % 2)
                        if ul == 0:
                            mi = len(pmmap)
                            pmmap[(j, p)] = mi
                            pm, pmn = PTm[mi % 2], 'PTm%d' % (mi % 2)
                            T.op('pe', lambda: PE.matmul(ps[5][0:16, 0:n], lhsT=KT[hs, S:S + 16], rhs=QT[hs, q0:q0 + n], start=True, stop=True),
                                 r=['KT', 'QT'], w=['ps5'])
                            if j == 0:
                                T.op('act', lambda: A.activation(out=PTmf[:, 0:n], in_=ps[5][0:16, 0:n], func=AF.Exp,
                                                                 bias=tabbc[0:16, 15 * 20 + hh:15 * 20 + hh + 1], scale=1.0),
                                     r=['ps5', 'tabbc'], w=['PTmf'])
                                T.op('act', lambda: A.activation(out=PTmf[:, 0:128], in_=ps[5][0:16, 0:128], func=AF.Exp),
                                     r=['ps5', 'PTmf'], w=['PTmf'])
                                T.op('dve', lambda: V.tensor_tensor(out=PTmf[:, 0:128], in0=PTmf[:, 0:128], in1=ETk0[p][:, :], op=ALU.mult),
                                     r=['PTmf', 'ETk0%d' % p], w=['PTmf'])
                                T.op('dve', lambda: V.tensor_copy(out=pm[:, 0:n], in_=PTmf[:, 0:n]), r=['PTmf'], w=[pmn])
                            elif j < 8:
                                T.op('act', lambda: A.activation(out=pm[:, 0:n], in_=ps[5][0:16, 0:n], func=AF.Exp,
                                                                 bias=tabbc[0:16, 15 * 20 + hh:15 * 20 + hh + 1], scale=1.0),
                                     r=['ps5', 'tabbc'], w=[pmn])
                            else:
                                T.op('act', lambda: A.activation(out=PTmf[:, 0:16], in_=ps[5][0:16, 0:16], func=AF.Exp), r=['ps5'], w=['PTmf'])
                                T.op('dve', lambda: V.tensor_tensor(out=pm[:, 0:16], in0=PTmf[:, 0:16], in1=ETmm[p][:, :], op=ALU.mult),
                                     r=['PTmf', 'ETmm%d' % p], w=[pmn])
                        if j < 8:
                            u = j * 4 + ul
                            qs = slice(u * 128, u * 128 + 128)
                            tl = [t for t in (u - 1, u, u + 1) if 0 <= t < 32]
                            c0 = (tl[0] - (u - 1)) * 128
                            w_ = len(tl) * 128
                            for si, t in enumerate(tl):
                                T.op('pe', lambda: PE.matmul(Sb[:, si * 128:(si + 1) * 128], lhsT=KT[hs, t * 128:(t + 1) * 128], rhs=QT[hs, qs],
                                                             start=True, stop=True), r=['KT', 'QT'], w=[Sn], sig=(si == len(tl) - 1))
                            T.op('act', lambda: A.activation(out=pf[:, 0:w_], in_=Sb[:, 0:w_], func=AF.Exp), r=[Sn], w=[pfn])
                            T.op('dve', lambda: V.tensor_tensor(out=P_[:, 0:w_], in0=pf[:, 0:w_], in1=ETo[p][:, c0:c0 + w_], op=ALU.mult),
                                 r=[pfn, 'ETo%d' % p], w=[Pn])
                        else:
                            T.op('pe', lambda: PE.matmul(Sb[:, 0:16], lhsT=KT[hs, 0:128], rhs=QT[hs, S:S + 16], start=True, stop=True),
                                 r=['KT', 'QT'], w=[Sn])
                            T.op('act', lambda: A.activation(out=pf[:, 0:16], in_=Sb[:, 0:16], func=AF.Exp), r=[Sn], w=[pfn])
                            T.op('dve', lambda: V.tensor_tensor(out=P_[:, 0:16], in0=pf[:, 0:16], in1=ETq0[p][:, :], op=ALU.mult),
                                 r=[pfn, 'ETq0%d' % p], w=[Pn])

                    def backo(i):
                        j, p, ul = its[i]
                        q0, n = CH512[j]
                        vt, vtn = (Vt, 'Vt') if p == 0 else (Vt2, 'Vt2')
                        Ob, On = ps[3 + p], 'ps%d' % (3 + p)
                        P_, Pn = PT[i % 3], 'PT%d' % (i % 3)
                        mi = pmmap[(j, p)]
                        pm, pmn = PTm[mi % 2], 'PTm%d' % (mi % 2)
                        sg, sgn = stg[j % 2], 'stgB%d' % (j % 2)
                        if j < 8:
                            u = j * 4 + ul
                            tl = [t for t in (u - 1, u, u + 1) if 0 <= t < 32]
                            osl = slice(ul * 128, ul * 128 + 128)
                            T.op('pe', lambda: PE.matmul(Ob[:, osl], lhsT=vt[0:16, 32, :], rhs=pm[:, osl], start=True, stop=False),
                                 r=[pmn, vtn], w=[On], sig=False)
                            for si, t in enumerate(tl):
                                last = (si == len(tl) - 1)
                                T.op('pe', lambda: PE.matmul(Ob[:, osl], lhsT=vt[:, t, :], rhs=P_[:, si * 128:(si + 1) * 128], start=False, stop=last),
                                     r=[Pn, vtn], w=[On], sig=last)
                            if ul < 3:
                                return
                        else:
                            T.op('pe', lambda: PE.matmul(Ob[:, 0:16], lhsT=vt[0:16, 32, :], rhs=pm[:, 0:16], start=True, stop=False),
                                 r=[pmn, vtn], w=[On], sig=False)
                            T.op('pe', lambda: PE.matmul(Ob[:, 0:16], lhsT=vt[:, 0, :], rhs=P_[:, 0:16], start=False, stop=True),
                                 r=[Pn, vtn], w=[On], sig=True)
                        os_ = slice(64 * p, 64 * p + 64)
                        rs_ = slice(64 * (1 - p), 64 * (1 - p) + 64)
                        sc = li * 16 + ch * 2 + p
                        T.op('dve', lambda: V.tensor_copy(out=rc[p][os_, 0:n], in_=Ob[rs_, 0:n]), r=[On], w=['rc%d' % p])
                        T.op('dve', lambda: V.tensor_scalar(out=rc[p][os_, 0:n], in0=rc[p][os_, 0:n], scalar1=esink[os_, sc:sc + 1], scalar2=None,
                                                            op0=ALU.add), r=['rc%d' % p, 'esink'], w=['rc%d' % p])
                        T.op('dve', lambda: V.reciprocal(out=rc[p][os_, 0:n], in_=rc[p][os_, 0:n]), r=['rc%d' % p], w=['rc%d' % p])
                        T.op('dve', lambda: V.tensor_tensor(out=sg[os_, 0:n], in0=Ob[os_, 0:n], in1=rc[p][os_, 0:n], op=ALU.mult),
                             r=[On, 'rc%d' % p, sgn], w=[sgn + '_%d' % p])
                        if p == 1:
                            T.dma('sp', mixT_d.ap()[ch * 128:(ch + 1) * 128, q0:q0 + n], sg[:, 0:n], r=[sgn + '_0', sgn + '_1'], w=[sgn])

                    for i in range(len(its) + LA):
                        if i < len(its):
                            fronto(i)
                        if i - LA >= 0:
                            backo(i - LA)
                T.barrier()

        def phase_c(layer, even, li, last):
            with ExitStack() as st:
                wout = sb(st, "wout", [128, 8, DM], BF16)
                wup = sb(st, "wup", [128, 8, DFF], BF16)
                wdown = sb(st, "wdown", [128, 32, DM], BF16)
                wo_src = dr["w_out%d" % layer].ap()
                wstg = [sb(st, "wstgC%d" % i, [128, 1024], F32) for i in range(2)]
                pc = 0
                for k in range(8):
                    sn = 'wstgC%d' % (pc % 2)
                    T.dma('sp', wstg[pc % 2][:, :], wo_src[k * 128:(k + 1) * 128, :], w=[sn])
                    cast(pc, wout[:, k, :], wstg[pc % 2][:, :], sn, 'wout_%d' % k)
                    pc += 1
                for k in range(8):
                    for q4 in range(4):
                        sn = 'wstgC%d' % (pc % 2)
                        T.dma('sp', wstg[pc % 2][:, :], dr["w_up%d" % layer].ap()[k * 128:(k + 1) * 128, q4 * 1024:(q4 + 1) * 1024], w=[sn])
                        cast(pc, wup[:, k, q4 * 1024:(q4 + 1) * 1024], wstg[pc % 2][:, :], sn, 'wup_%d_%d' % (k, q4))
                        pc += 1
                for f in range(32):
                    sn = 'wstgC%d' % (pc % 2)
                    T.dma('sp', wstg[pc % 2][:, :], dr["w_down%d" % layer].ap()[f * 128:(f + 1) * 128, :], w=[sn])
                    cast(pc, wdown[:, f, :], wstg[pc % 2][:, :], sn, 'wdown_%d' % f)
                    pc += 1
                NB = 1 if last else 2
                hc = [sb(st, "hcC%d" % i, [128, 8, 256], F32) for i in range(NB)]
                mx = [sb(st, "mxC%d" % i, [128, 8, 256], BF16) for i in range(NB)]
                hn = sb(st, "hnC", [128, 8, 256], BF16)
                u = sb(st, "uC", [128, 32, 256], BF16)
                rstd = sb(st, "rstdC", [128, 256], F32)
                rl = [sb(st, "rlC%d" % i, [128, 256], F32) for i in range(2)]
                if last:
                    yo = sb(st, "yo", [128, 8, 256], F32)
                    ot = [sb(st, "ot%d" % i, [128, DM], F32) for i in range(1)]
                hT_v = hT_d.ap().rearrange("(k p) t -> p k t", p=128)
                mT_v = mixT_d.ap().rearrange("(k p) t -> p k t", p=128)
                pi = 0
                for ci, (t0, n) in enumerate(CH256):
                    b = ci % NB
                    hcn, mxn = 'hcC%d' % b, 'mxC%d' % b
                    T.dma('sp', hc[b][:, :, 0:n], hT_v[:, :, t0:t0 + n], w=[hcn + '_%d' % k for k in range(8)] + [hcn])
                    T.dma('sp', mx[b][:, :, 0:n], mT_v[:, :, t0:t0 + n], w=[mxn])
                    for dc in range(8):
                        pb, pn = ps[1 + pi % 6], 'ps%d' % (1 + pi % 6)
                        pi += 1
                        for k in range(8):
                            T.op('pe', lambda k=k: PE.matmul(pb[:, 0:n], lhsT=wout[:, k, dc * 128:(dc + 1) * 128], rhs=mx[b][:, k, 0:n],
                                                             start=(k == 0), stop=(k == 7)), r=[mxn, 'wout_%d' % k], w=[pn], sig=(k == 7))
                        T.op('dve', lambda: V.tensor_tensor(out=hc[b][:, dc, 0:n], in0=pb[:, 0:n], in1=hc[b][:, dc, 0:n], op=ALU.add),
                             r=[pn, hcn + '_%d' % dc], w=[hcn + '_%d' % dc])
                    hk_all = [hcn + '_%d' % k for k in range(8)]
                    for k in range(8):
                        T.op('act' if k % 2 == 0 else 'pool',
                             (lambda k=k: A.activation(out=u[:, k, 0:n], in_=hc[b][:, k, 0:n], func=AF.Square)) if k % 2 == 0 else
                             (lambda k=k: G.tensor_tensor(out=u[:, k, 0:n], in0=hc[b][:, k, 0:n], in1=hc[b][:, k, 0:n], op=ALU.mult)),
                             r=[hcn + '_%d' % k], w=['uC_%d' % k])
                    for k in range(8):
                        T.op('pe', lambda k=k: PE.matmul(ps[0][:, 0:n], lhsT=onesD[:, :], rhs=u[:, k, 0:n], start=(k == 0), stop=(k == 7)),
                             r=['uC_%d' % k, 'onesD'], w=['ps0'], sig=(k == 7))
                    T.op('act', lambda: A.activation(out=rstd[:, 0:n], in_=ps[0][:, 0:n], func=AF.Sqrt, bias=cst[:, 0:1], scale=1.0),
                         r=['ps0', 'cst'], w=['rstd'])
                    T.op('dve', lambda: V.reciprocal(out=rstd[:, 0:n], in_=rstd[:, 0:n]), r=['rstd'], w=['rstd'])
                    gcol = 32 + layer * 8
                    for k in range(8):
                        T.op('dve', lambda k=k: V.scalar_tensor_tensor(out=hn[:, k, 0:n], in0=hc[b][:, k, 0:n], scalar=pvec[:, gcol + k:gcol + k + 1],
                                                                       in1=rstd[:, 0:n], op0=ALU.mult, op1=ALU.mult),
                             r=[hcn + '_%d' % k, 'rstd', 'pvec'], w=['hnC_%d' % k])
                    hn_r = ['hnC_%d' % k for k in range(8)]
                    for f in range(32):
                        pb, pn = ps[1 + pi % 6], 'ps%d' % (1 + pi % 6)
                        pi += 1
                        for k in range(8):
                            T.op('pe', lambda k=k: PE.matmul(pb[:, 0:n], lhsT=wup[:, k, f * 128:(f + 1) * 128], rhs=hn[:, k, 0:n],
                                                             start=(k == 0), stop=(k == 7)), r=hn_r + ['wup_%d_%d' % (k, f // 8)], w=[pn], sig=(k == 7))
                        r_, rn = rl[f % 2], 'rlC%d' % (f % 2)
                        T.op('act', lambda: A.activation(out=r_[:, 0:n], in_=pb[:, 0:n], func=AF.Relu), r=[pn], w=[rn])
                        T.op('pool', lambda: G.tensor_tensor(out=u[:, f, 0:n], in0=r_[:, 0:n], in1=r_[:, 0:n], op=ALU.mult),
                             r=[rn], w=['uC_%d' % f])
                    u_r = ['uC_%d' % f for f in range(32)]
                    for dc in range(8):
                        pb, pn = ps[1 + pi % 6], 'ps%d' % (1 + pi % 6)
                        pi += 1
                        for f in range(32):
                            T.op('pe', lambda f=f: PE.matmul(pb[:, 0:n], lhsT=wdown[:, f, dc * 128:(dc + 1) * 128], rhs=u[:, f, 0:n],
                                                             start=(f == 0), stop=(f == 31)), r=u_r + ['wdown_%d' % f], w=[pn], sig=(f == 31))
                        T.op('dve', lambda: V.tensor_tensor(out=hc[b][:, dc, 0:n], in0=pb[:, 0:n], in1=hc[b][:, dc, 0:n], op=ALU.add),
                             r=[pn, hcn + '_%d' % dc], w=[hcn + '_%d' % dc])
                    if not last or dbg:
                        T.dma('sp', hT_v[:, :, t0:t0 + n], hc[b][:, :, 0:n], r=hk_all + [hcn])
                    if last and t0 < S:
                        for k in range(8):
                            T.op('act' if k % 2 == 0 else 'pool',
                                 (lambda k=k: A.activation(out=u[:, k, 0:n], in_=hc[b][:, k, 0:n], func=AF.Square)) if k % 2 == 0 else
                                 (lambda k=k: G.tensor_tensor(out=u[:, k, 0:n], in0=hc[b][:, k, 0:n], in1=hc[b][:, k, 0:n], op=ALU.mult)),
                                 r=[hcn + '_%d' % k], w=['uC_%d' % k])
                        for k in range(8):
                            T.op('pe', lambda k=k: PE.matmul(ps[0][:, 0:n], lhsT=onesD[:, :], rhs=u[:, k, 0:n], start=(k == 0), stop=(k == 7)),
                                 r=['uC_%d' % k, 'onesD'], w=['ps0'], sig=(k == 7))
                        T.op('act', lambda: A.activation(out=rstd[:, 0:n], in_=ps[0][:, 0:n], func=AF.Sqrt, bias=cst[:, 0:1], scale=1.0),
                             r=['ps0', 'cst'], w=['rstd'])
                        T.op('dve', lambda: V.reciprocal(out=rstd[:, 0:n], in_=rstd[:, 0:n]), r=['rstd'], w=['rstd'])
                        for k in range(8):
                            T.op('dve', lambda k=k: V.scalar_tensor_tensor(out=yo[:, k, 0:n], in0=hc[b][:, k, 0:n], scalar=pvec[:, 64 + k:65 + k],
                                                                           in1=rstd[:, 0:n], op0=ALU.mult, op1=ALU.mult),
                                 r=[hcn + '_%d' % k, 'rstd', 'pvec'], w=['yo_%d' % k])
                        for tt in range(n // 128):
                            o_, on_ = ot[0], 'ot0'
                            for k in range(8):
                                pb, pn = ps[1 + pi % 6], 'ps%d' % (1 + pi % 6)
                                pi += 1
                                T.op('pe', lambda: PE.transpose(pb[:, 0:128], yo[:, k, tt * 128:(tt + 1) * 128], ident[:, :]),
                                     r=['yo_%d' % k, 'ident'], w=[pn])
                                if k % 2 == 0:
                                    T.op('dve', lambda: V.tensor_copy(out=o_[:, k * 128:(k + 1) * 128], in_=pb[:, 0:128]), r=[pn, on_], w=[on_ + '_%d' % k])
                                else:
                                    T.op('act', lambda: A.copy(out=o_[:, k * 128:(k + 1) * 128], in_=pb[:, 0:128]), r=[pn, on_], w=[on_ + '_%d' % k])
                            T.dma('sp', out_d.ap()[t0 + tt * 128:t0 + (tt + 1) * 128, :], o_[:, :],
                                  r=[on_ + '_%d' % k for k in range(8)], w=[on_])
                T.barrier()

        @block.sync
        def _(sync):
            body()
    print('program instructions (incl waits):', T.ninstr, 'ops', T.nops, 'waits', T.nwaits, 'cnt', T.cnt, flush=True)
    return nc


def host_constants():
    cmat = np.zeros((4, 128, 128), np.float32)
    cmat[0] = np.eye(128, dtype=np.float32)
    cmat[1] = np.eye(128, dtype=np.float32)[::-1]
    perm = np.zeros((128, 128), np.float32)
    for dp in range(128):
        d = dp + 16 if (dp % 32) < 16 else dp - 16
        perm[d, dp] = 1.0
    cmat[2] = perm
    rel = np.arange(-RB, RB)
    bk = rel_bucket_np(rel)
    oh = np.zeros((32, 2048), np.float32)
    oh[bk, np.arange(2048)] = 1.0
    msk = np.broadcast_to((np.abs(rel) <= 128).astype(np.float32)[None], (20, 2048)).copy()
    tok = np.arange(L)
    real = tok < S
    rows = np.where(real, tok // 64, 0).astype(np.float32)
    cols = np.where(real, tok % 64, 0).astype(np.float32)
    inv = (np.float32(10000.0) ** (-np.arange(0, 32, 2, dtype=np.float32) / np.float32(32))).astype(np.float32)
    ang_r = rows[:, None] * inv[None]
    ang_c = cols[:, None] * inv[None]
    cosT = np.zeros((64, L), np.float32)
    sinT = np.zeros((64, L), np.float32)
    for d in range(64):
        ang = ang_r if d < 32 else ang_c
        i = d % 16
        first = (d % 32) < 16
        cosT[d] = np.cos(ang[:, i])
        sinT[d] = (-np.sin(ang[:, i])) if first else np.sin(ang[:, i])
    rope = np.stack([np.concatenate([cosT, cosT], 0), np.concatenate([sinT, sinT], 0)], 0).astype(np.float32)
    return cmat, oh, msk, rope


def host_layout(inp):
    cmat, oh, msk, rope = host_constants()
    we = inp["w_in_even"]
    kb = we[:, :, 2048:2176]
    wine = np.concatenate([we[:, :, 0:512], we[:, :, 512:1024], we[:, :, 1536:2048],
                           kb[:, :, 0:64], kb[:, :, 0:64], kb[:, :, 64:128], kb[:, :, 64:128],
                           we[:, :, 1024:1536], we[:, :, 2176:2304]], axis=2)
    wo = inp["w_in_odd"]
    kc = wo[:, :, 1024:1152]
    wino = np.concatenate([wo[:, :, 0:1024], kc[:, :, 0:64], kc[:, :, 0:64], kc[:, :, 64:128], kc[:, :, 64:128],
                           wo[:, :, 1152:1280]], axis=2)
    pvec = np.zeros((128, 96), np.float32)
    for i in range(4):
        pvec[:, i * 8:(i + 1) * 8] = inp["norm_attn"][i].reshape(8, 128).T
        pvec[:, 32 + i * 8:32 + (i + 1) * 8] = inp["norm_mlp"][i].reshape(8, 128).T
    pvec[:, 64:72] = inp["norm_final"].reshape(8, 128).T
    for e in range(2):
        pvec[:, 72 + e] = inp["diff_subln"][e]
        pvec[:, 74 + 2 * e] = np.concatenate([inp["qk_norm"][e, 0], inp["qk_norm"][e, 0]])
        pvec[:, 75 + 2 * e] = np.concatenate([inp["qk_norm"][e, 1], inp["qk_norm"][e, 1]])
    shared = {
        "meta": np.ascontiguousarray(inp["meta_tokens"], np.float32),
        "rel_table": np.ascontiguousarray(inp["rel_table"], np.float32),
        "pvec": pvec,
        "diff_lambda": np.ascontiguousarray(inp["diff_lambda"], np.float32).reshape(1, 512),
        "sinks": np.ascontiguousarray(inp["sinks"], np.float32).reshape(1, 32),
        "cmat": cmat, "onehot": oh, "bandmask": msk, "rope": rope,
    }
    wts = {}
    for ly in range(4):
        wts["w_in%d" % ly] = np.ascontiguousarray((wine if ly % 2 == 0 else wino)[ly // 2], np.float32)
        wts["w_out%d" % ly] = np.ascontiguousarray(inp["w_out_even" if ly % 2 == 0 else "w_out_odd"][ly // 2], np.float32)
        wts["w_up%d" % ly] = np.ascontiguousarray(inp["w_up"][ly], np.float32)
        wts["w_down%d" % ly] = np.ascontiguousarray(inp["w_down"][ly], np.float32)
    shared["_wts"] = wts
    return shared


_NC_CACHE = {}
LAUNCH_GROUPS = [[0, 1, 2, 3]]


def kernel(**inputs):
    inp = {k: np.asarray(v) for k, v in inputs.items()}
    shared = host_layout(inp)
    x = np.ascontiguousarray(inp["x"], np.float32)
    nb = x.shape[0]
    meta = shared.pop("meta")
    wts = shared.pop("_wts")
    hT = None
    out = None
    for gi, grp in enumerate(LAUNCH_GROUPS):
        key = tuple(grp)
        if key not in _NC_CACHE:
            _NC_CACHE[key] = build_program(layers=list(grp), total_layers=4)
        nc = _NC_CACHE[key]
        in_maps = []
        for b in range(nb):
            m = dict(shared)
            for ly in grp:
                for nm in ("w_in", "w_out", "w_up", "w_down"):
                    m["%s%d" % (nm, ly)] = wts["%s%d" % (nm, ly)]
            if grp[0] == 0:
                m["x"] = x[b]
                m["meta"] = meta
            else:
                m["hT_in"] = hT[b]
            in_maps.append(m)
        res = run_bass_kernel_spmd(nc, in_maps, core_ids=list(range(nb)))
        if grp[-1] == 3:
            out = np.stack([np.asarray(r["out"], np.float32) for r in res.results], axis=0)
        else:
            hT = [np.ascontiguousarray(np.asarray(r["hT"], np.float32)) for r in res.results]
    return out
```
